# Optimizing a Trainium2 kernel written in Bass

```python
import jax, jax.numpy as jnp
from jax import lax
import numpy as np

D_MODEL = 1024
BATCH = 4
SEQ = 4096
DEPTH = 1
DEC_BATCH = 128
DEC_SEQ = 8
PAST_LEN = 16384
PAGE_SIZE = 128

N_HEADS = 8
QK_NOPE = 64
ROPE_DIM = 32
V_DIM = 64
Q_RANK = 384
KV_RANK = 256
CONV_DIM = 512
CONV_WIDTH = 31
D_ATTN = N_HEADS * V_DIM
D_MIX = D_ATTN + CONV_DIM
IN_COLS = Q_RANK + KV_RANK + ROPE_DIM + 2 * CONV_DIM
D_FF = 2816
ROPE_THETA = 10000.0
Q_BLOCK = 128
EPS = 1e-6
SM_SCALE = (QK_NOPE + ROPE_DIM) ** -0.5
NEG_INF = -1e30

kernel_name = "mla_conformer_hybrid_step"


def rms_norm(x, g):
    xf = x.astype(jnp.float32)
    y = xf * lax.rsqrt(jnp.mean(xf * xf, axis=-1, keepdims=True) + EPS)
    return (y * g.astype(jnp.float32)).astype(x.dtype)


def layer_norm(x, g, b):
    xf = x.astype(jnp.float32)
    mu = jnp.mean(xf, axis=-1, keepdims=True)
    var = jnp.mean(jnp.square(xf - mu), axis=-1, keepdims=True)
    y = (xf - mu) * lax.rsqrt(var + EPS)
    return (y * g.astype(jnp.float32) + b.astype(jnp.float32)).astype(x.dtype)


def swiglu(x, wg, wu, wd):
    return (jax.nn.silu(x @ wg) * (x @ wu)) @ wd


def rope_tables(pos, dtype):
    inv_freq = ROPE_THETA ** (-jnp.arange(0, ROPE_DIM, 2, dtype=jnp.float32) / ROPE_DIM)
    ang = pos.astype(jnp.float32)[:, None] * inv_freq[None, :]
    return jnp.cos(ang).astype(dtype), jnp.sin(ang).astype(dtype)


def apply_rope(x, cos, sin):
    x1, x2 = jnp.split(x, 2, axis=-1)
    return jnp.concatenate([x1 * cos - x2 * sin, x1 * sin + x2 * cos], axis=-1)


def attend_prompt(q_abs, q_pe, c_kv, k_pe):
    B, S, H, C = q_abs.shape
    nb = S // Q_BLOCK
    qa = q_abs.reshape(B, nb, Q_BLOCK, H, C).swapaxes(0, 1)
    qp = q_pe.reshape(B, nb, Q_BLOCK, H, ROPE_DIM).swapaxes(0, 1)
    kpos = jnp.arange(S)

    def one_block(args):
        blk, qa_b, qp_b = args
        s = (jnp.einsum('bthc,bsc->bhts', qa_b, c_kv)
             + jnp.einsum('bthr,bsr->bhts', qp_b, k_pe)).astype(jnp.float32) * SM_SCALE
        qpos = blk * Q_BLOCK + jnp.arange(Q_BLOCK)
        s = jnp.where(qpos[:, None] >= kpos[None, :], s, NEG_INF)
        p = jax.nn.softmax(s, axis=-1).astype(c_kv.dtype)
        return jnp.einsum('bhts,bsc->bthc', p, c_kv)

    o = lax.map(one_block, (jnp.arange(nb), qa, qp))
    return o.swapaxes(0, 1).reshape(B, S, H, C)


def attend_sample(q_abs, q_pe, c_kv, k_pe, cache_kv, cache_kr, page_table):
    DB, T = q_abs.shape[0], q_abs.shape[1]
    c_past = cache_kv[page_table].reshape(DB, -1, KV_RANK)
    r_past = cache_kr[page_table].reshape(DB, -1, ROPE_DIM)
    P = c_past.shape[1]
    s_past = jnp.einsum('bthc,bsc->bhts', q_abs, c_past) + jnp.einsum('bthr,bsr->bhts', q_pe, r_past)
    s_new = jnp.einsum('bthc,bsc->bhts', q_abs, c_kv) + jnp.einsum('bthr,bsr->bhts', q_pe, k_pe)
    causal = jnp.tril(jnp.ones((T, T), dtype=bool))
    s_new = jnp.where(causal, s_new.astype(jnp.float32) * SM_SCALE, NEG_INF)
    s = jnp.concatenate([s_past.astype(jnp.float32) * SM_SCALE, s_new], axis=-1)
    p = jax.nn.softmax(s, axis=-1).astype(c_kv.dtype)
    return (jnp.einsum('bhts,bsc->bthc', p[..., :P], c_past)
            + jnp.einsum('bhts,bsc->bthc', p[..., P:], c_kv))


def hybrid_layer(x, pos, conv_state, attend, lp):
    x = x + 0.5 * swiglu(rms_norm(x, lp['g_ffn1']), lp['w1_gate'], lp['w1_up'], lp['w1_down'])
    h = rms_norm(x, lp['g_mix'])
    u = h @ lp['w_in']
    o1 = Q_RANK
    o2 = o1 + KV_RANK
    o3 = o2 + ROPE_DIM
    q_c, kv_c, kr, conv_in = u[..., :o1], u[..., o1:o2], u[..., o2:o3], u[..., o3:]
    cos, sin = rope_tables(pos, x.dtype)
    q = jnp.einsum('btq,qhd->bthd', rms_norm(q_c, lp['g_q']), lp['w_q_b'])
    q_nope, q_pe = q[..., :QK_NOPE], apply_rope(q[..., QK_NOPE:], cos[:, None, :], sin[:, None, :])
    c_kv = rms_norm(kv_c, lp['g_kv'])
    k_pe = apply_rope(kr, cos, sin)
    w_uk = lp['w_kv_b'][..., :QK_NOPE]
    w_uv = lp['w_kv_b'][..., QK_NOPE:]
    q_abs = jnp.einsum('bthn,chn->bthc', q_nope, w_uk)
    o_lat = attend(q_abs, q_pe, c_kv, k_pe)
    attn_out = jnp.einsum('bthc,chv->bthv', o_lat, w_uv).reshape(x.shape[0], x.shape[1], D_ATTN)
    ga, gb = conv_in[..., :CONV_DIM], conv_in[..., CONV_DIM:]
    g = ga * jax.nn.sigmoid(gb)
    ext = jnp.concatenate([conv_state, g], axis=1)
    y_dw = lax.conv_general_dilated(
        ext, lp['w_dw'][:, None, :], window_strides=(1,), padding='VALID',
        dimension_numbers=('NWC', 'WIO', 'NWC'), feature_group_count=CONV_DIM) + lp['b_dw']
    conv_out = jax.nn.silu(layer_norm(y_dw, lp['g_cn'], lp['b_cn']))
    new_conv = ext[:, -(CONV_WIDTH - 1):]
    mix = jnp.concatenate([rms_norm(attn_out, lp['g_out_attn']), rms_norm(conv_out, lp['g_out_conv'])], axis=-1)
    x = x + mix @ lp['w_out']
    x = x + 0.5 * swiglu(rms_norm(x, lp['g_ffn2']), lp['w2_gate'], lp['w2_up'], lp['w2_down'])
    return x, c_kv, k_pe, new_conv


def setup_inputs(seed: int = 0) -> dict:
    key = jax.random.key(seed)
    ks = jax.random.split(key, 40)
    f32 = jnp.float32
    n_pages = PAST_LEN // PAGE_SIZE
    n_used = DEC_BATCH * n_pages
    n_phys = (n_used * 5) // 4

    def nrm(k, shape, scale):
        return jax.random.normal(k, shape, f32) * scale

    def gain(k, shape):
        return 1.0 + 0.01 * jax.random.normal(k, shape, f32)

    L = DEPTH
    perm = jax.random.permutation(ks[5], n_phys)[:n_used]
    return {
        'x_prompt': nrm(ks[0], (BATCH, SEQ, D_MODEL), 1.0),
        'x_sample': nrm(ks[1], (DEC_BATCH, DEC_SEQ, D_MODEL), 1.0),
        'cache_kv_latent': nrm(ks[2], (L, n_phys, PAGE_SIZE, KV_RANK), 1.0),
        'cache_k_rope': nrm(ks[3], (L, n_phys, PAGE_SIZE, ROPE_DIM), 1.0),
        'state_conv': nrm(ks[4], (L, DEC_BATCH, CONV_WIDTH - 1, CONV_DIM), 0.5),
        'page_table': perm.reshape(DEC_BATCH, n_pages).astype(jnp.int32),
        'g_ffn1': gain(ks[6], (L, D_MODEL)),
        'w1_gate': nrm(ks[7], (L, D_MODEL, D_FF), D_MODEL ** -0.5),
        'w1_up': nrm(ks[8], (L, D_MODEL, D_FF), D_MODEL ** -0.5),
        'w1_down': nrm(ks[9], (L, D_FF, D_MODEL), D_FF ** -0.5),
        'g_mix': gain(ks[10], (L, D_MODEL)),
        'w_in': nrm(ks[11], (L, D_MODEL, IN_COLS), D_MODEL ** -0.5),
        'g_q': gain(ks[12], (L, Q_RANK)),
        'w_q_b': nrm(ks[13], (L, Q_RANK, N_HEADS, QK_NOPE + ROPE_DIM), Q_RANK ** -0.5),
        'g_kv': gain(ks[14], (L, KV_RANK)),
        'w_kv_b': nrm(ks[15], (L, KV_RANK, N_HEADS, QK_NOPE + V_DIM), KV_RANK ** -0.5),
        'w_dw': nrm(ks[16], (L, CONV_WIDTH, CONV_DIM), CONV_WIDTH ** -0.5),
        'b_dw': nrm(ks[17], (L, CONV_DIM), 0.01),
        'g_cn': gain(ks[18], (L, CONV_DIM)),
        'b_cn': nrm(ks[19], (L, CONV_DIM), 0.01),
        'g_out_attn': gain(ks[20], (L, D_ATTN)),
        'g_out_conv': gain(ks[21], (L, CONV_DIM)),
        'w_out': nrm(ks[22], (L, D_MIX, D_MODEL), D_MIX ** -0.5),
        'g_ffn2': gain(ks[23], (L, D_MODEL)),
        'w2_gate': nrm(ks[24], (L, D_MODEL, D_FF), D_MODEL ** -0.5),
        'w2_up': nrm(ks[25], (L, D_MODEL, D_FF), D_MODEL ** -0.5),
        'w2_down': nrm(ks[26], (L, D_FF, D_MODEL), D_FF ** -0.5),
        'g_final': gain(ks[27], (D_MODEL,)),
    }


def reference(x_prompt, x_sample, cache_kv_latent, cache_k_rope, state_conv, page_table,
              g_ffn1, w1_gate, w1_up, w1_down, g_mix, w_in, g_q, w_q_b, g_kv, w_kv_b,
              w_dw, b_dw, g_cn, b_cn, g_out_attn, g_out_conv, w_out,
              g_ffn2, w2_gate, w2_up, w2_down, g_final):
    B, S = x_prompt.shape[0], x_prompt.shape[1]
    DB, T = x_sample.shape[0], x_sample.shape[1]
    past_len = page_table.shape[1] * PAGE_SIZE
    pos_prompt = jnp.arange(S)
    pos_sample = past_len + jnp.arange(T)

    xp, xs = x_prompt, x_sample
    kv_p, kr_p, cv_p, kv_s, kr_s, cv_s = [], [], [], [], [], []
    for l in range(DEPTH):
        lp = dict(g_ffn1=g_ffn1[l], w1_gate=w1_gate[l], w1_up=w1_up[l], w1_down=w1_down[l],
                  g_mix=g_mix[l], w_in=w_in[l], g_q=g_q[l], w_q_b=w_q_b[l], g_kv=g_kv[l],
                  w_kv_b=w_kv_b[l], w_dw=w_dw[l], b_dw=b_dw[l], g_cn=g_cn[l], b_cn=b_cn[l],
                  g_out_attn=g_out_attn[l], g_out_conv=g_out_conv[l], w_out=w_out[l],
                  g_ffn2=g_ffn2[l], w2_gate=w2_gate[l], w2_up=w2_up[l], w2_down=w2_down[l])
        zero_state = jnp.zeros((B, CONV_WIDTH - 1, CONV_DIM), dtype=xp.dtype)
        xp, ckv, kpe, ncv = hybrid_layer(xp, pos_prompt, zero_state, attend_prompt, lp)
        kv_p.append(ckv.reshape(B, S // PAGE_SIZE, PAGE_SIZE, KV_RANK))
        kr_p.append(kpe.reshape(B, S // PAGE_SIZE, PAGE_SIZE, ROPE_DIM))
        cv_p.append(ncv)

        ck_l, cr_l = cache_kv_latent[l], cache_k_rope[l]

        def attend_s(q_abs, q_pe, c_kv, k_pe, ck_l=ck_l, cr_l=cr_l):
            return attend_sample(q_abs, q_pe, c_kv, k_pe, ck_l, cr_l, page_table)

        xs, ckv, kpe, ncv = hybrid_layer(xs, pos_sample, state_conv[l], attend_s, lp)
        kv_s.append(ckv)
        kr_s.append(kpe)
        cv_s.append(ncv)

    y_prompt = rms_norm(xp, g_final)
    y_sample = rms_norm(xs, g_final)
    return (y_prompt, y_sample, jnp.stack(kv_p), jnp.stack(kr_p), jnp.stack(cv_p),
            jnp.stack(kv_s), jnp.stack(kr_s), jnp.stack(cv_s))
```

```python
import contextlib
import numpy as np
import concourse.bass as bass
import concourse.mybir as mybir
from concourse.bass_utils import run_bass_kernel_spmd

F32, BF16, I32 = mybir.dt.float32, mybir.dt.bfloat16, mybir.dt.int32
AF = mybir.ActivationFunctionType
ALU = mybir.AluOpType
AX = mybir.AxisListType

D = 1024
DFF = 2816
NF = DFF // 128
QR, KVR, RD, CD = 384, 256, 32, 512
NH = 8
EPS = 1e-6
SM_SCALE = 96.0 ** -0.5
CW = 31
HALO = CW - 1


class Buf:
    def __init__(self, t):
        self.t = t
        self.sem = None
        self.cnt = 0
        self.last_w = None
        self.readers = []

    def __getitem__(self, k):
        return self.t[k]


class Prog:
    ENG = ("pe", "act", "dve", "pool", "sp")

    def __init__(self, nc, es):
        self.nc = nc
        self.es = es
        self.lists = {e: [] for e in self.ENG}
        self.count = {e: 0 for e in self.ENG}
        self.esem = {e: es.enter_context(nc.semaphore("es_" + e)) for e in self.ENG}
        self.seen = {e: {} for e in self.ENG}
        self.semobj = {}
        for e in self.ENG:
            self.semobj[id(self.esem[e])] = self.esem[e]
        self.nsem = 0
        self.out_tokens = []

    def bufsem(self, b):
        if b.sem is None:
            b.sem = self.es.enter_context(self.nc.semaphore("bs%d" % self.nsem))
            self.nsem += 1
            self.semobj[id(b.sem)] = b.sem
        return b.sem

    def op(self, eng, fn, reads=(), writes=(), dma=None, is_out=False, extra=()):
        deps = set(extra)
        for b in reads:
            if b.last_w is not None:
                deps.add(b.last_w)
        for b in writes:
            if b.last_w is not None:
                deps.add(b.last_w)
            deps.update(b.readers)
        waits = []
        dmax = {}
        for (sid, val) in deps:
            dmax[sid] = max(dmax.get(sid, 0), val)
        for (sid, val) in dmax.items():
            if eng == "pe" and dma is None and sid == id(self.esem["pe"]):
                continue
            if self.seen[eng].get(sid, 0) >= val:
                continue
            self.seen[eng][sid] = val
            waits.append((self.semobj[sid], val))
        if dma is not None:
            sem = self.bufsem(dma)
            dma.cnt += 16
            tok = (id(sem), dma.cnt)
            inc = 16
        else:
            sem = self.esem[eng]
            self.count[eng] += 1
            tok = (id(sem), self.count[eng])
            inc = 1
        self.lists[eng].append((waits, fn, sem, inc))
        for b in writes:
            b.last_w = tok
            b.readers = []
        for b in reads:
            b.readers.append(tok)
        if is_out:
            self.out_tokens.append(tok)
        return tok

    def finish(self):
        need = {}
        for (sid, val) in self.out_tokens:
            need[sid] = max(need.get(sid, 0), val)
        waits = [(self.semobj[s], v) for s, v in need.items()]
        self.lists["sp"].append((waits, None, None, 0))

    def replay(self, eng, e):
        for waits, fn, sem, inc in self.lists[eng]:
            for (s, v) in waits:
                e.wait_ge(s, v)
            if fn is None:
                continue
            ins = fn(e)
            ins.then_inc(sem, inc)


def build(NT, NPG, NPHYS, dbg=False):
    NOWN = NT // 2
    NGRP = NT // 4
    NG8 = NPG // 8
    nc = bass.Bass("TRN2", target_bir_lowering=False)
    es = contextlib.ExitStack()

    def din(name, shape, dt=F32):
        return nc.dram_tensor(name, list(shape), dt, kind="ExternalInput")

    def dout(name, shape, dt=F32):
        return nc.dram_tensor(name, list(shape), dt, kind="ExternalOutput")

    xp = din("xp", [NT * 128, D])
    xs = din("xs", [128, D])
    ckv = din("ckv", [NPHYS * 16, 8 * KVR])
    ckr = din("ckr", [NPHYS * 16, 8 * RD])
    ptab = din("ptab", [16, NPG], I32)
    state = din("state", [16, HALO, CD])
    WD = {}
    for nm, shp in (("w1g", [D, DFF]), ("w1u", [D, DFF]), ("w1d", [DFF, D]), ("w2g", [D, DFF]), ("w2u", [D, DFF]), ("w2d", [DFF, D]),
                    ("w_in", [D, 1696]), ("w_out", [D, D])):
        WD[nm] = din(nm, shp)
    w_qb = din("w_qb", [QR, NH * 96])
    w_kvb = din("w_kvb", [KVR, NH * 128])
    w_dw = din("w_dw", [CW, CD])
    gvec = {}
    for nm, n in (("g_ffn1", D), ("g_mix", D), ("g_ffn2", D), ("g_final", D), ("g_q", QR), ("g_kv", KVR),
                  ("b_dw", CD), ("g_cn", CD), ("b_cn", CD), ("g_oa", CD), ("g_oc", CD)):
        gvec[nm] = din(nm, [1, n])
    cs_p = din("cs_p", [NT * 128, 64])
    cs_pT = din("cs_pT", [64, NOWN * 128])
    selsum_d = din("selsum", [128, 64])
    cs_s = din("cs_s", [128, 64])
    cs_sT = din("cs_sT", [64, 128])
    ident_d = din("ident", [128, 128])
    tril_d = din("tril", [128, 128])
    m0_d = din("m0", [128, 1])
    pmod_d = din("pmod", [128, 1], I32)
    smask_d = din("smask", [128, 16 * 64])

    y_p = dout("y_p", [NOWN * 128, D]); y_s = dout("y_s", [128, D])
    kv_p = dout("kv_p", [NOWN * 128, KVR]); kr_p = dout("kr_p", [NOWN * 128, RD])
    conv_p = dout("conv_p", [HALO, CD])
    kv_s = dout("kv_s", [128, KVR]); kr_s = dout("kr_s", [128, RD])
    conv_s = dout("conv_s", [16, HALO, CD])

    P = Prog(nc, es)

    def sb(name, shape, dt=F32):
        return Buf(es.enter_context(nc.sbuf_tensor("s_" + name, list(shape), dt)))

    def ps(name, shape, dt=F32):
        return Buf(es.enter_context(nc.psum_tensor("p_" + name, list(shape), dt)))

    def alias(parent, view):
        b = Buf(view)
        b.last_w = parent.last_w
        b.readers = list(parent.readers)
        return b

    PB = [ps("pb%d" % i, [128, 512]) for i in range(4)]
    PT0 = ps("pt0", [128, 1024], BF16)
    SPST = ps("spst", [128, 512])
    SPT = ps("spt", [128, 1024], BF16)
    SACC = ps("sacc", [128, 512])

    ident = sb("ident", [128, 128]); identb = sb("identb", [128, 128], BF16)
    tril = sb("tril", [128, 128]); trilb = sb("trilb", [128, 4, 128], BF16)
    m0 = sb("m0", [128, 1])
    onesb = sb("onesb", [128, 128], BF16)
    zerob = sb("zerob", [128, 512], BF16)
    onesf = sb("onesf", [128, 128])
    selsum = sb("selsum", [128, 64])

    def dma_in(dst, dst_ap, src_ap, eng="sp"):
        P.op(eng, lambda e, o=dst_ap, i=src_ap: e.dma_start(out=o, in_=i), writes=[dst], dma=dst)

    def out_dma(dst_ap, src, src_ap):
        P.op("pool", lambda e: e.dma_start(out=dst_ap, in_=src_ap), reads=[src], dma=src, is_out=True)

    dma_in(ident, ident[:], ident_d.ap())
    dma_in(tril, tril[:], tril_d.ap())
    dma_in(m0, m0[:], m0_d.ap())
    dma_in(selsum, selsum[:], selsum_d.ap())
    P.op("dve", lambda e: e.tensor_copy(identb[:], ident[:]), reads=[ident], writes=[identb])
    for h in range(4):
        P.op("dve", lambda e, h=h: e.tensor_copy(trilb[:, h, :], tril[:]), reads=[tril], writes=[trilb])
    P.op("pool", lambda e: e.memset(onesb[:], 1.0), writes=[onesb])
    P.op("pool", lambda e: e.memset(zerob[:], 0.0), writes=[zerob])
    P.op("pool", lambda e: e.memset(onesf[:], 1.0), writes=[onesf])

    def bvec(nm, n):
        b = sb("bv_" + nm, [128, n])
        dma_in(b, b[:], gvec[nm].ap()[0:1, :].partition_broadcast(128), eng="pool")
        return b
    g_kv_b = bvec("g_kv", KVR); g_cn_b = bvec("g_cn", CD); b_cn_b = bvec("b_cn", CD)
    g_fin_b = bvec("g_final", D)

    def cvec(nm, n):
        b = sb("cv_" + nm, [128, n // 128])
        P.op("sp", lambda e: e.dma_start(out=b[:], in_=gvec[nm].ap().rearrange("o (c p) -> p (o c)", p=128),
                                         allow_slow_non_contiguous=True), writes=[b], dma=b)
        return b
    gc = {nm: cvec(nm, n) for nm, n in (("g_ffn1", D), ("g_mix", D), ("g_ffn2", D), ("g_q", QR), ("b_dw", CD),
                                        ("g_oa", CD), ("g_oc", CD))}
    wdw = sb("wdw", [128, 4, CW])
    for c in range(4):
        P.op("sp", lambda e, c=c: e.dma_start(out=wdw[:, c, :], in_=w_dw.ap()[:, c * 128:(c + 1) * 128].rearrange("k p -> p k"),
                                              allow_slow_non_contiguous=True), writes=[wdw], dma=wdw)

    NSTG, NWB = 4, 8
    stg = [sb("stg%d" % i, [128, 1024]) for i in range(NSTG)]
    wbf = [sb("wbf%d" % i, [128, 1024], BF16) for i in range(NWB)]
    gxt = sb("gxt", [128, 8, 128])
    rr = {"stg": 0, "wb": 0, "cast": 0}
    NPIECE = 6 * NF + 16 + 8
    scr = nc.dram_tensor("scr", [NPIECE, 128, 1024], BF16, kind="Internal")
    piece_idx = {}
    piece_tok = {}

    def set_gx(col):
        for c in range(8):
            P.op("pool", lambda e, c=c: e.tensor_scalar(gxt[:, c, :], onesf[:], col[:, c:c + 1], None, op0=ALU.mult),
                 reads=[col, onesf], writes=[gxt])

    def _to_scratch(key, w):
        idx = len(piece_idx)
        piece_idx[key] = idx
        piece_tok[key] = P.op("sp", lambda e: e.dma_start(out=scr.ap()[idx], in_=w[:, :]), reads=[w], dma=w)

    specs = []
    for f in range(NF):
        specs += [("col", "w1g", f * 128, 128, "g_ffn1"), ("col", "w1u", f * 128, 128, "g_ffn1"), ("row", "w1d", f * 128, None, None)]
    for c0 in (0, 128, 256, 384, 512):
        specs.append(("col", "w_in", c0, 128, "g_mix"))
    specs.append(("col", "w_in", 640, 32, "g_mix"))
    for cc in range(8):
        specs.append(("col", "w_in", 672 + cc * 128, 128, "g_mix"))
    for k in range(8):
        specs.append(("row", "w_out", k * 128, None, "g_oa" if k < 4 else "g_oc"))
    for f in range(NF):
        specs += [("col", "w2g", f * 128, 128, "g_ffn2"), ("col", "w2u", f * 128, 128, "g_ffn2"), ("row", "w2d", f * 128, None, None)]
    LOOK = 3
    loaded = {}

    def prep_load(j):
        kind, Wn, off, ncols, gname = specs[j]
        W = WD[Wn]
        s_ = stg[rr["stg"] % NSTG]; rr["stg"] += 1
        if kind == "col":
            sv = s_.t[:].rearrange("p (c j) -> p c j", j=128)
            src = W.ap()[:, off:off + ncols].rearrange("(c p) j -> p c j", p=128)
            P.op("sp", lambda e: e.dma_start(out=sv[:, :, 0:ncols], in_=src), writes=[s_], dma=s_)
        else:
            P.op("sp", lambda e: e.dma_start(out=s_[:, :], in_=W.ap()[off:off + 128, :]), writes=[s_], dma=s_)
        loaded[j] = s_

    cur_gx = {"name": None}

    def prep_cast_store(j):
        kind, Wn, off, ncols, gname = specs[j]
        s_ = loaded.pop(j)
        w = wbf[rr["wb"] % NWB]; rr["wb"] += 1
        if kind == "col":
            if cur_gx["name"] != gname:
                set_gx(gc[gname])
                cur_gx["name"] = gname
            sv = s_.t[:].rearrange("p (c j) -> p c j", j=128)
            wv = w.t[:].rearrange("p (c j) -> p c j", j=128)
            rr["cast"] += 1
            ce = "pool" if rr["cast"] % 3 == 0 else "dve"
            P.op(ce, lambda e: e.tensor_tensor(wv[:, :, 0:ncols], sv[:, :, 0:ncols], gxt[:, :, 0:ncols], ALU.mult), reads=[s_, gxt], writes=[w])
        elif gname is not None:
            k = off // 128
            sb_ = gc[gname]
            scol = sb_[:, (k % 4):(k % 4) + 1]
            P.op("act", lambda e: e.activation(w[:, :], s_[:, :], AF.Copy, scale=scol), reads=[s_, sb_], writes=[w])
        else:
            P.op("act", lambda e: e.copy(w[:, :], s_[:, :]), reads=[s_], writes=[w])
        _to_scratch((Wn, off), w)

    for j in range(min(LOOK, len(specs))):
        prep_load(j)
    for i in range(len(specs)):
        prep_cast_store(i)
        if i + LOOK < len(specs):
            prep_load(i + LOOK)

    def piece(Wn, off, col=False):
        w = wbf[rr["wb"] % NWB]; rr["wb"] += 1
        idx = piece_idx[(Wn, off)]
        P.op("sp", lambda e: e.dma_start(out=w[:, :], in_=scr.ap()[idx]), writes=[w], dma=w, extra=[piece_tok[(Wn, off)]])
        if col:
            return w, w.t[:].rearrange("p (c j) -> p c j", j=128)
        return w

    wq = sb("wq", [128, 3, NH, 128], BF16)
    for c in range(3):
        s_ = stg[rr["stg"] % NSTG]; rr["stg"] += 1
        P.op("sp", lambda e, c=c, s_=s_: e.dma_start(out=s_[:, 0:NH * 96], in_=w_qb.ap()[c * 128:(c + 1) * 128, :]), writes=[s_], dma=s_)
        sv = s_.t[:, 0:NH * 96].rearrange("p (h j) -> p h j", j=96)
        sc = gc["g_q"][:, c:c + 1]
        P.op("dve", lambda e, c=c, sc=sc, sv=sv: e.tensor_scalar(wq[:, c, :, 0:96], sv[:, :, :], sc, None, op0=ALU.mult),
             reads=[s_, gc["g_q"]], writes=[wq])
        P.op("dve", lambda e, c=c, sc=sc, sv=sv: e.tensor_scalar(wq[:, c, :, 96:112], sv[:, :, 80:96], sc, -1.0, op0=ALU.mult, op1=ALU.mult),
             reads=[s_, gc["g_q"]], writes=[wq])
        P.op("dve", lambda e, c=c, sc=sc, sv=sv: e.tensor_scalar(wq[:, c, :, 112:128], sv[:, :, 64:80], sc, None, op0=ALU.mult),
             reads=[s_, gc["g_q"]], writes=[wq])
    wuv = sb("wuv", [128, 2, NH, 64], BF16)
    wukT = sb("wukT", [64, NH, 256], BF16)
    for c in range(2):
        s_ = stg[rr["stg"] % NSTG]; rr["stg"] += 1
        P.op("sp", lambda e, c=c, s_=s_: e.dma_start(out=s_[:, :], in_=w_kvb.ap()[c * 128:(c + 1) * 128, :]), writes=[s_], dma=s_)
        sv = s_.t[:, :].rearrange("p (h j) -> p h j", j=128)
        P.op("dve", lambda e, c=c, sv=sv: e.tensor_copy(wuv[:, c, :, :], sv[:, :, 64:128]), reads=[s_], writes=[wuv])
        for h in range(NH):
            pb = PB[h % 2]
            P.op("pe", lambda e, h=h, pb=pb, sv=sv: e.transpose(pb[0:64, 0:128], sv[:, h, 0:64], ident[:]),
                 reads=[s_, ident], writes=[pb])
            P.op("act", lambda e, h=h, c=c, pb=pb: e.copy(wukT[:, h, c * 128:(c + 1) * 128], pb[0:64, 0:128]),
                 reads=[pb], writes=[wukT])
    gT_s = sb("gT_s", [128, 4, 16 * 38], BF16)
    for g4 in range(4):
        s_ = stg[rr["stg"] % NSTG]; rr["stg"] += 1
        P.op("sp", lambda e, g4=g4, s_=s_: e.dma_start(out=s_[0:120, 0:CD], in_=state.ap()[g4 * 4:(g4 + 1) * 4].rearrange("b s c -> (b s) c")), writes=[s_], dma=s_)
        for cc in range(4):
            pb = PB[(g4 * 4 + cc) % 2]
            P.op("pe", lambda e, s_=s_, cc=cc, pb=pb: e.transpose(pb[:, 0:120], s_[0:120, cc * 128:(cc + 1) * 128], ident[0:120, 0:120]),
                 reads=[s_, ident], writes=[pb])
            P.op("act", lambda e, g4=g4, cc=cc, pb=pb: e.copy(
                gT_s.t[:, cc, g4 * 4 * 38:(g4 + 1) * 4 * 38].rearrange("p (b s) -> p b s", s=38)[:, :, 0:HALO],
                pb.t[:, 0:120].rearrange("p (b s) -> p b s", s=HALO)), reads=[pb], writes=[gT_s])
    smask = sb("smask", [128, 16 * 64], BF16)
    s_m = stg[rr["stg"] % NSTG]; rr["stg"] += 1
    dma_in(s_m, s_m[:, :], smask_d.ap())
    P.op("dve", lambda e: e.tensor_copy(smask[:], s_m[:, :]), reads=[s_m], writes=[smask])

    NCG = 4
    cg = [alias(stg[i], stg[i].t[:, :].bitcast(BF16).rearrange("p (t c) -> p t c", c=KVR)) for i in range(NCG)]
    rg = [sb("rg%d" % i, [128, 8, RD], BF16) for i in range(NCG)]
    cTs = [sb("cTs%d" % i, [128, 2, 256], BF16) for i in range(2)]
    rTs = [sb("rTs%d" % i, [128, 128], BF16) for i in range(2)]
    pTs = [sb("pTs%d" % i, [128, 8, 64], BF16) for i in range(2)]
    junk = sb("junk", [128, D], BF16)

    TMAX = 4
    X = [sb("X%d" % i, [128, D]) for i in range(TMAX + 1)]
    XS = 4
    xnT = sb("xnT", [128, 8, TMAX * 128], BF16)
    xnb = sb("xnb", [128, D], BF16)
    st_ss = [sb("ss%d" % i, [128, 4]) for i in range(4)]
    rrn = {"n": 0}
    sg = [sb("sg%d" % i, [128, 512]) for i in range(2)]
    hm = [sb("hm%d" % i, [128, 2, 512], BF16) for i in range(2)]
    sgl = sg[0]
    ybuf = sb("ybuf", [128, D])

    def rstd_of(src_ap, n, reads, ssb):
        P.op("act", lambda e: e.activation(junk[:, 0:n], src_ap, AF.Square, accum_out=ssb[:, 0:1]), reads=reads, writes=[junk, ssb])
        P.op("dve", lambda e: e.tensor_scalar(ssb[:, 1:2], ssb[:, 0:1], 1.0 / n, EPS, op0=ALU.mult, op1=ALU.add), reads=[ssb], writes=[ssb])
        P.op("act", lambda e: e.activation(ssb[:, 2:3], ssb[:, 1:2], AF.Sqrt), reads=[ssb], writes=[ssb])
        P.op("dve", lambda e: e.reciprocal(ssb[:, 3:4], ssb[:, 2:3]), reads=[ssb], writes=[ssb])

    def next_ss():
        ssb = st_ss[rrn["n"] % 4]; rrn["n"] += 1
        return ssb

    def norm_T(xb, slot):
        ssb = next_ss()
        rstd_of(xb[:, :], D, [xb], ssb)
        P.op("act", lambda e: e.activation(xnb[:, :], xb[:, :], AF.Copy, scale=ssb[:, 3:4]), reads=[xb, ssb], writes=[xnb])

        def tr(e):
            for c in range(8):
                ins = e.transpose(PT0[:, c * 128:(c + 1) * 128], xnb[:, c * 128:(c + 1) * 128], identb[:])
            return ins
        P.op("pe", tr, reads=[xnb, identb], writes=[PT0])
        P.op("act", lambda e: e.copy(xnT[:, :, slot * 128:(slot + 1) * 128], PT0.t[:].rearrange("p (c j) -> p c j", j=128)),
             reads=[PT0], writes=[xnT])

    def ffn(tiles, wk, pump=None):
        T = len(tiles)
        NTOK = T * 128
        for i, t in enumerate(tiles):
            norm_T(X[t], i)
        for fp in range(NF // 2):
            hb = hm[fp % 2]
            wds = []
            for k in range(2):
                f = fp * 2 + k
                wgb, wgv = piece(wk + "g", f * 128, True)
                wub, wuv_ = piece(wk + "u", f * 128, True)
                wds.append(piece(wk + "d", f * 128))
                pg, pu = PB[0], PB[1]

                def mm(e, wv=wgv, pp=pg):
                    for c in range(8):
                        ins = e.matmul(pp[:, 0:NTOK], wv[:, c, :], xnT[:, c, 0:NTOK], start=(c == 0), stop=(c == 7))
                    return ins
                P.op("pe", mm, reads=[wgb, xnT], writes=[pg])
                if pump is not None:
                    pump()

                def mm2(e, wv=wuv_, pp=pu):
                    for c in range(8):
                        ins = e.matmul(pp[:, 0:NTOK], wv[:, c, :], xnT[:, c, 0:NTOK], start=(c == 0), stop=(c == 7))
                    return ins
                P.op("pe", mm2, reads=[wub, xnT], writes=[pu])
                if pump is not None:
                    pump()
                sgb = sg[f % 2]
                P.op("act", lambda e, sgb=sgb: e.activation(sgb[:, 0:NTOK], pg[:, 0:NTOK], AF.Silu), reads=[pg], writes=[sgb])
                P.op("dve", lambda e, sgb=sgb, hb=hb, k=k: e.tensor_tensor(hb[:, k, 0:NTOK], sgb[:, 0:NTOK], pu[:, 0:NTOK], ALU.mult),
                     reads=[sgb, pu], writes=[hb])
                if pump is not None:
                    pump()
            for i, t in enumerate(tiles):
                def mmd(e, i=i, hb=hb, wds=wds):
                    for dh in range(2):
                        for k in range(2):
                            ins = e.matmul(PB[2 + dh][:, :], hb[:, k, i * 128:(i + 1) * 128], wds[k][:, dh * 512:(dh + 1) * 512],
                                           start=(k == 0), stop=(k == 1))
                    return ins
                P.op("pe", mmd, reads=[hb] + wds, writes=[PB[2], PB[3]])
                for dh in range(2):
                    P.op("dve", lambda e, t=t, dh=dh: e.scalar_tensor_tensor(
                        X[t][:, dh * 512:(dh + 1) * 512], PB[2 + dh][:, :], 0.5, X[t][:, dh * 512:(dh + 1) * 512], ALU.mult, ALU.add),
                        reads=[PB[2 + dh], X[t]], writes=[X[t]])
                if pump is not None:
                    pump()

    NKT = NT
    NRC = (NKT + 2) // 3
    cT = sb("cT", [128, 2, NKT * 128], BF16)
    cTr = sb("cTr", [96, NRC * 128], BF16)
    cnat = sb("cnat", [128, NKT, 256], BF16)
    cT_s = sb("cT_s", [128, 2, 128], BF16); cTr_s = sb("cTr_s", [32, 128], BF16); cnat_s = sb("cnat_s", [128, 256], BF16)
    gT = sb("gT", [128, 4, HALO + TMAX * 128], BF16)
    ukv = sb("ukv", [128, 288])
    ckvf = sb("ckvf", [128, KVR])
    ckvb = sb("ckvb", [128, 288], BF16)
    kpe = sb("kpe", [128, RD])
    cstab = sb("cstab", [128, 64])
    tmp32 = sb("tmp32", [128, 32])
    qcn = sb("qcn", [128, QR]); qcnb = sb("qcnb", [128, QR], BF16)
    qcnT = sb("qcnT", [128, 3, 128], BF16)
    qh = sb("qh", [128, NH, 128])
    qnb = sb("qnb", [64, NH, 128], BF16)
    qabsT = sb("qabsT", [128, 2, NH * 128], BF16)
    qpeT = sb("qpeT", [96, NH * 128], BF16)
    qabsT_s = sb("qabsT_s", [128, 2, NH * 128], BF16)
    qpeT_s = sb("qpeT_s", [96, NH * 128], BF16)
    cs128 = sb("cs128", [128, 128])
    pT = [sb("pT%d" % i, [128, 512], BF16) for i in range(2)]
    oTb = sb("oTb", [128, 2, 512], BF16)
    oTs = sb("oTs", [128, 2, NH * 128], BF16)
    osb = sb("osb", [128, 192])
    lacc = sb("lacc", [128, 8])
    attn = sb("attn", [128, 512])
    yTs = [sb("yT%d" % i, [128, 128]) for i in range(4)]
    ydw = sb("ydw", [128, CD])
    cvo = sb("cvo", [128, CD])
    mix = xnb
    mixT = sb("mixT", [128, 8, 128], BF16)
    stat = sb("stat", [128, 8])
    gtok = ydw
    gtk2 = cvo

    def in_proj(slots, own_slots, kv_tile0, cs_src, is_sample, pump=None):
        n = len(slots)
        NTOK = n * 128
        kvp = [piece("w_in", c0, True) for c0 in (384, 512, 640)]
        for i, s in enumerate(slots):
            pk = PB[2 + i % 2]

            def mm(e, i=i, pk=pk):
                for j, (wb_, wv) in enumerate(kvp):
                    ncols = 128 if j < 2 else 32
                    for c in range(8):
                        ins = e.matmul(pk[:, j * 128:j * 128 + ncols], xnT[:, c, i * 128:(i + 1) * 128], wv[:, c, 0:ncols],
                                       start=(c == 0), stop=(c == 7))
                return ins
            P.op("pe", mm, reads=[xnT] + [w for w, _ in kvp], writes=[pk])
            if pump is not None:
                for _ in range(3):
                    pump()
            P.op("act", lambda e, pk=pk: e.copy(ukv[:, 0:288], pk[:, 0:288]), reads=[pk], writes=[ukv])
            ssb = next_ss()
            rstd_of(ukv[:, 0:KVR], KVR, [ukv], ssb)
            P.op("dve", lambda e, ssb=ssb: e.scalar_tensor_tensor(ckvf[:, :], ukv[:, 0:KVR], ssb[:, 3:4], g_kv_b[:, :], ALU.mult, ALU.mult),
                 reads=[ukv, ssb, g_kv_b], writes=[ckvf])
            tok0 = (kv_tile0 + i) * 128 if not is_sample else 0
            dma_in(cstab, cstab[:], cs_src.ap()[tok0:tok0 + 128, :])
            P.op("dve", lambda e: e.tensor_tensor(kpe[:, :], ukv[:, 256:288], cstab[:, 0:32], ALU.mult), reads=[ukv, cstab], writes=[kpe])
            P.op("dve", lambda e: e.tensor_tensor(tmp32[:, 0:16], ukv[:, 272:288], cstab[:, 32:48], ALU.mult), reads=[ukv, cstab], writes=[tmp32])
            P.op("dve", lambda e: e.tensor_tensor(tmp32[:, 16:32], ukv[:, 256:272], cstab[:, 48:64], ALU.mult), reads=[ukv, cstab], writes=[tmp32])
            P.op("dve", lambda e: e.tensor_sub(kpe[:, 0:16], kpe[:, 0:16], tmp32[:, 0:16]), reads=[tmp32, kpe], writes=[kpe])
            P.op("dve", lambda e: e.tensor_add(kpe[:, 16:32], kpe[:, 16:32], tmp32[:, 16:32]), reads=[tmp32, kpe], writes=[kpe])
            if is_sample:
                out_dma(kv_s.ap(), ckvf, ckvf[:, :])
                out_dma(kr_s.ap(), kpe, kpe[:, :])
            elif s in own_slots:
                ot = (kv_tile0 + i) // 2
                out_dma(kv_p.ap()[ot * 128:(ot + 1) * 128, :], ckvf, ckvf[:, :])
                out_dma(kr_p.ap()[ot * 128:(ot + 1) * 128, :], kpe, kpe[:, :])
            kt = kv_tile0 + i
            P.op("pool", lambda e: e.tensor_copy(ckvb[:, 0:256], ckvf[:, :]), reads=[ckvf], writes=[ckvb])
            P.op("pool", lambda e: e.tensor_copy(ckvb[:, 256:288], kpe[:, :]), reads=[kpe], writes=[ckvb])
            if is_sample:
                d_nat, d_natap = cnat_s, cnat_s[:, :]
                d_T, d_Tap = cT_s, cT_s[:, :, :]
                d_r, d_rap = cTr_s, cTr_s[0:32, :]
            else:
                d_nat, d_natap = cnat, cnat[:, kt, :]
                d_T, d_Tap = cT, cT[:, :, kt * 128:(kt + 1) * 128]
                d_r, d_rap = cTr, cTr[32 * (kt % 3):32 * (kt % 3) + 32, (kt // 3) * 128:(kt // 3 + 1) * 128]
            P.op("pool", lambda e, d=d_natap: e.tensor_copy(d, ckvb[:, 0:256]), reads=[ckvb], writes=[d_nat])
            rb = 32 * (kt % 3) if not is_sample else 0

            def tr(e, rb=rb):
                e.transpose(PT0[:, 0:128], ckvb[:, 0:128], identb[:])
                e.transpose(PT0[:, 128:256], ckvb[:, 128:256], identb[:])
                return e.transpose(PT0[rb:rb + 32, 256:384], ckvb[:, 256:288], identb[:])
            P.op("pe", tr, reads=[ckvb, identb], writes=[PT0])
            P.op("act", lambda e, d=d_Tap: e.copy(d, PT0.t[:, 0:256].rearrange("p (c j) -> p c j", j=128)), reads=[PT0], writes=[d_T])
            P.op("act", lambda e, d=d_rap, rb=rb: e.copy(d, PT0[rb:rb + 32, 256:384]), reads=[PT0], writes=[d_r])
        gdst = gT_s if is_sample else gT
        for cc in range(4):
            wa, wav = piece("w_in", 672 + cc * 128, True)
            wb_, wbv = piece("w_in", 1184 + cc * 128, True)
            pa, pbk = PB[0 + (cc % 2) * 2], PB[1 + (cc % 2) * 2]

            def mma(e, wv=wav, pp=pa):
                for c in range(8):
                    ins = e.matmul(pp[:, 0:NTOK], wv[:, c, :], xnT[:, c, 0:NTOK], start=(c == 0), stop=(c == 7))
                return ins
            P.op("pe", mma, reads=[wa, xnT], writes=[pa])

            def mmb(e, wv=wbv, pp=pbk):
                for c in range(8):
                    ins = e.matmul(pp[:, 0:NTOK], wv[:, c, :], xnT[:, c, 0:NTOK], start=(c == 0), stop=(c == 7))
                return ins
            P.op("pe", mmb, reads=[wb_, xnT], writes=[pbk])
            if pump is not None:
                for _ in range(2):
                    pump()
            P.op("act", lambda e, pbk=pbk: e.activation(sgl[:, 0:NTOK], pbk[:, 0:NTOK], AF.Sigmoid), reads=[pbk], writes=[sgl])
            if is_sample:
                P.op("dve", lambda e, cc=cc, pa=pa: e.tensor_tensor(
                    gT_s.t[:, cc, :].rearrange("p (b s) -> p b s", s=38)[:, :, HALO:38],
                    sgl.t[:, 0:128].rearrange("p (b t) -> p b t", t=8), pa.t[:, 0:128].rearrange("p (b t) -> p b t", t=8), ALU.mult),
                    reads=[sgl, pa], writes=[gT_s])
                P.op("dve", lambda e, cc=cc, pa=pa: e.tensor_tensor(gtok[:, cc * 128:(cc + 1) * 128], sgl[:, 0:128], pa[:, 0:128], ALU.mult),
                     reads=[sgl, pa], writes=[gtok])
            else:
                P.op("dve", lambda e, cc=cc, pa=pa: e.tensor_tensor(gT[:, cc, HALO:HALO + NTOK], sgl[:, 0:NTOK], pa[:, 0:NTOK], ALU.mult),
                     reads=[sgl, pa], writes=[gT])

    def q_path(pos, csT_src, own_idx, qa_dst, qp_dst):
        qp = [piece("w_in", c0, True) for c0 in (0, 128, 256)]
        pq = PB[0]

        def mm(e):
            for j, (wb_, wv) in enumerate(qp):
                for c in range(8):
                    ins = e.matmul(pq[:, j * 128:(j + 1) * 128], xnT[:, c, pos * 128:(pos + 1) * 128], wv[:, c, :], start=(c == 0), stop=(c == 7))
            return ins
        P.op("pe", mm, reads=[xnT] + [w for w, _ in qp], writes=[pq])
        P.op("act", lambda e: e.copy(qcn[:, :], pq[:, 0:QR]), reads=[pq], writes=[qcn])
        ssb = next_ss()
        rstd_of(qcn[:, :], QR, [qcn], ssb)
        P.op("dve", lambda e: e.tensor_scalar(qcnb[:, :], qcn[:, :], ssb[:, 3:4], None, op0=ALU.mult), reads=[qcn, ssb], writes=[qcnb])

        def tr(e):
            for c in range(3):
                ins = e.transpose(PT0[:, c * 128:(c + 1) * 128], qcnb[:, c * 128:(c + 1) * 128], identb[:])
            return ins
        P.op("pe", tr, reads=[qcnb, identb], writes=[PT0])
        P.op("act", lambda e: e.copy(qcnT[:, :, :], PT0.t[:, 0:384].rearrange("p (c j) -> p c j", j=128)), reads=[PT0], writes=[qcnT])
        dma_in(cs128, cs128[64:128, :], csT_src.ap()[:, own_idx * 128:(own_idx + 1) * 128])
        for hh in range(2):
            pb = PB[2 + hh]

            def mmq(e, hh=hh, pb=pb):
                for h4 in range(4):
                    h = hh * 4 + h4
                    for c in range(3):
                        ins = e.matmul(pb[:, h4 * 128:(h4 + 1) * 128], wq[:, c, h, :], qcnT[:, c, :], start=(c == 0), stop=(c == 2))
                return ins
            P.op("pe", mmq, reads=[wq, qcnT], writes=[pb])
            P.op("act", lambda e, hh=hh, pb=pb: e.copy(qh[:, hh * 4:(hh + 1) * 4, :], pb.t[:].rearrange("p (h j) -> p h j", j=128)),
                 reads=[pb], writes=[qh])
        P.op("dve", lambda e: e.tensor_copy(qnb[:, :, :], qh[0:64, :, :]), reads=[qh], writes=[qnb])
        for h in range(NH):
            P.op("dve", lambda e, h=h: e.tensor_tensor(qh[64:128, h, :], qh[64:128, h, :], cs128[64:128, :], ALU.mult),
                 reads=[qh, cs128], writes=[qh])
        for hh in range(2):
            pr = PB[2 + hh]
            P.op("pe", lambda e, hh=hh, pr=pr: e.matmul(pr[0:96, 0:512], selsum3[64:128, :],
                                                       qh.t[64:128, hh * 4:(hh + 1) * 4, :].rearrange("p h j -> p (h j)"), start=True, stop=True),
                 reads=[qh, selsum3], writes=[pr])
            P.op("act", lambda e, hh=hh, pr=pr: e.copy(qp_dst[0:96, hh * 512:(hh + 1) * 512], pr[0:96, 0:512]), reads=[pr], writes=[qp_dst])
        for ch in range(2):
            for hh in range(2):
                pb = PB[(ch * 2 + hh) % 2]

                def mma(e, ch=ch, hh=hh, pb=pb):
                    for h4 in range(4):
                        h = hh * 4 + h4
                        ins = e.matmul(pb[:, h4 * 128:(h4 + 1) * 128], wukT[:, h, ch * 128:(ch + 1) * 128], qnb[:, h, :], start=True, stop=True)
                    return ins
                P.op("pe", mma, reads=[wukT, qnb], writes=[pb])
                P.op("act", lambda e, ch=ch, hh=hh, pb=pb: e.copy(qa_dst[:, ch, hh * 512:(hh + 1) * 512], pb[:, :]), reads=[pb], writes=[qa_dst])

    selsum3 = sb("selsum3", [128, 96])
    for r in range(3):
        P.op("dve", lambda e, r=r: e.tensor_copy(selsum3[:, 32 * r:32 * r + 32], selsum[:, 0:32]), reads=[selsum], writes=[selsum3])

    def attend_prompt(L, pump=None):
        for hh in range(2):
            attend_half(L, hh, pump)

    def attend_half(L, hh, pump):
        po = [PB[0], PB[1]]
        plpa = PB[3]

        def z(e):
            e.matmul(po[0][:, :], zerob[:, 0:128], zerob[:, 0:512], start=True, stop=True)
            e.matmul(po[1][:, :], zerob[:, 0:128], zerob[:, 0:512], start=True, stop=True)
            return e.matmul(plpa[:, 0:8], zerob[:, 0:128], zerob[:, 0:8], start=True, stop=True)
        P.op("pe", z, reads=[zerob], writes=[po[0], po[1], plpa])
        for kt in range(L + 2):
            if kt <= L:
                attend_qk(L, hh, kt)
            if kt >= 1:
                attend_pv(L, hh, kt - 1, po, plpa)
            if pump is not None and kt % 2 == 1:
                pump()
        P.op("act", lambda e: e.copy(oTb[:, 0, :], po[0][:, :]), reads=[po[0]], writes=[oTb])
        P.op("act", lambda e: e.copy(oTb[:, 1, :], po[1][:, :]), reads=[po[1]], writes=[oTb])
        P.op("dve", lambda e: e.reciprocal(lacc[:, hh * 4:(hh + 1) * 4], plpa[:, 0:4]), reads=[plpa], writes=[lacc])

        def ao(e):
            for h4 in range(4):
                h = hh * 4 + h4
                for ch in range(2):
                    ins = e.matmul(plpa[:, 256 + h4 * 64:256 + (h4 + 1) * 64], oTb[:, ch, h4 * 128:(h4 + 1) * 128], wuv[:, ch, h, :], start=(ch == 0), stop=(ch == 1))
            return ins
        P.op("pe", ao, reads=[oTb, wuv], writes=[plpa])
        for h4 in range(4):
            h = hh * 4 + h4
            P.op("dve", lambda e, h=h, h4=h4: e.tensor_scalar(attn[:, h * 64:(h + 1) * 64], plpa[:, 256 + h4 * 64:256 + (h4 + 1) * 64], lacc[:, h:h + 1], None, op0=ALU.mult),
                 reads=[plpa, lacc], writes=[attn])

    def attend_qk(L, hh, kt):
        pst = PB[2]
        pTb = pT[kt % 2]
        rb = 32 * (kt % 3)
        rc = (kt // 3) * 128

        def qk(e):
            e.matmul(pst[:, :], cT[:, 0, kt * 128:(kt + 1) * 128], qabsT[:, 0, hh * 512:(hh + 1) * 512], start=True, stop=False)
            e.matmul(pst[:, :], cT[:, 1, kt * 128:(kt + 1) * 128], qabsT[:, 1, hh * 512:(hh + 1) * 512], start=False, stop=False)
            return e.matmul(pst[:, :], cTr[rb:rb + 32, rc:rc + 128], qpeT[rb:rb + 32, hh * 512:(hh + 1) * 512], start=False, stop=True)
        P.op("pe", qk, reads=[cT, cTr, qabsT, qpeT], writes=[pst])
        P.op("act", lambda e: e.activation(pTb[:, :], pst[:, :], AF.Exp, scale=SM_SCALE), reads=[pst], writes=[pTb])
        if kt == L:
            P.op("dve", lambda e: e.tensor_tensor(pTb.t[:, :].rearrange("p (h j) -> p h j", j=128), pTb.t[:, :].rearrange("p (h j) -> p h j", j=128),
                                                  trilb[:, :, :], ALU.mult), reads=[pTb, trilb], writes=[pTb])
        if kt == 0:
            P.op("dve", lambda e: e.tensor_scalar(pTb[:, :], pTb[:, :], m0[:, 0:1], None, op0=ALU.mult), reads=[pTb, m0], writes=[pTb])

    def attend_pv(L, hh, kt, po, plpa):
        pTb = pT[kt % 2]

        def pv(e):
            e.matmul(po[0][:, :], cnat[:, kt, 0:128], pTb[:, :], start=False, stop=(kt == L), skip_group_check=True)
            e.matmul(po[1][:, :], cnat[:, kt, 128:256], pTb[:, :], start=False, stop=(kt == L), skip_group_check=True)
            for h4 in range(4):
                ins = e.matmul(plpa[:, h4:h4 + 1], pTb[:, h4 * 128:(h4 + 1) * 128], onesb[:, 0:1], start=False, stop=(kt == L), skip_group_check=True)
            return ins
        P.op("pe", pv, reads=[cnat, pTb, onesb], writes=[po[0], po[1], plpa])

    def yT_view(cc, is_sample):
        if is_sample:
            return yTs[cc].t[:, :].rearrange("p (b t) -> p b t", t=8)
        return yTs[cc][:, :]

    def conv_post(xslot, g_in_ap_fn, gsrc, conv_is_sample):
        for cc in range(4):
            P.op("dve", lambda e, cc=cc: e.tensor_scalar(yT_view(cc, conv_is_sample), g_in_ap_fn(cc, 0), wdw[:, cc, 0:1], gc["b_dw"][:, cc:cc + 1], op0=ALU.mult, op1=ALU.add),
                 reads=[gsrc, wdw, gc["b_dw"]], writes=[yTs[cc]])
        for k in range(1, CW):
            for cc in range(4):
                P.op("dve", lambda e, cc=cc, k=k: e.scalar_tensor_tensor(yT_view(cc, conv_is_sample), g_in_ap_fn(cc, k), wdw[:, cc, k:k + 1], yT_view(cc, conv_is_sample), ALU.mult, ALU.add),
                     reads=[gsrc, wdw, yTs[cc]], writes=[yTs[cc]])
        pc = PB[0]

        def tr(e):
            for cc in range(4):
                ins = e.transpose(pc[:, cc * 128:(cc + 1) * 128], yTs[cc][:, :], ident[:])
            return ins
        P.op("pe", tr, reads=yTs + [ident], writes=[pc])
        P.op("act", lambda e: e.copy(ydw[:, :], pc[:, :]), reads=[pc], writes=[ydw])
        P.op("dve", lambda e: e.reduce_sum(stat[:, 0:1], ydw[:, :], AX.X), reads=[ydw], writes=[stat])
        P.op("dve", lambda e: e.tensor_scalar(stat[:, 1:2], stat[:, 0:1], -1.0 / CD, None, op0=ALU.mult), reads=[stat], writes=[stat])
        P.op("dve", lambda e: e.tensor_scalar(ydw[:, :], ydw[:, :], stat[:, 1:2], None, op0=ALU.add), reads=[ydw, stat], writes=[ydw])
        ssb = next_ss()
        rstd_of(ydw[:, :], CD, [ydw], ssb)
        P.op("dve", lambda e: e.scalar_tensor_tensor(ydw[:, :], ydw[:, :], ssb[:, 3:4], g_cn_b[:, :], ALU.mult, ALU.mult), reads=[ydw, ssb, g_cn_b], writes=[ydw])
        P.op("dve", lambda e: e.tensor_add(ydw[:, :], ydw[:, :], b_cn_b[:, :]), reads=[ydw, b_cn_b], writes=[ydw])
        P.op("act", lambda e: e.activation(cvo[:, :], ydw[:, :], AF.Silu), reads=[ydw], writes=[cvo])
        ssa = next_ss()
        rstd_of(attn[:, :], CD, [attn], ssa)
        P.op("dve", lambda e: e.tensor_scalar(mix[:, 0:512], attn[:, :], ssa[:, 3:4], None, op0=ALU.mult), reads=[attn, ssa], writes=[mix])
        ssc = next_ss()
        rstd_of(cvo[:, :], CD, [cvo], ssc)
        P.op("dve", lambda e: e.tensor_scalar(mix[:, 512:1024], cvo[:, :], ssc[:, 3:4], None, op0=ALU.mult), reads=[cvo, ssc], writes=[mix])

        def trm(e):
            for c in range(8):
                ins = e.transpose(PT0[:, c * 128:(c + 1) * 128], mix[:, c * 128:(c + 1) * 128], identb[:])
            return ins
        P.op("pe", trm, reads=[mix, identb], writes=[PT0])
        P.op("act", lambda e: e.copy(mixT[:, :, :], PT0.t[:].rearrange("p (c j) -> p c j", j=128)), reads=[PT0], writes=[mixT])
        pd0, pd1 = PB[2], PB[3]
        for k in range(8):
            wk_ = piece("w_out", k * 128)

            def mmo(e, k=k, wk_=wk_):
                e.matmul(pd0[:, :], mixT[:, k, :], wk_[:, 0:512], start=(k == 0), stop=(k == 7))
                return e.matmul(pd1[:, :], mixT[:, k, :], wk_[:, 512:1024], start=(k == 0), stop=(k == 7))
            P.op("pe", mmo, reads=[mixT, wk_], writes=[pd0, pd1])
        for dh, pd in enumerate((pd0, pd1)):
            P.op("dve", lambda e, dh=dh, pd=pd: e.tensor_add(X[xslot][:, dh * 512:(dh + 1) * 512], X[xslot][:, dh * 512:(dh + 1) * 512], pd[:, :]),
                 reads=[pd, X[xslot]], writes=[X[xslot]])

    def final_out(xslot, dst_ap):
        ssb = next_ss()
        rstd_of(X[xslot][:, :], D, [X[xslot]], ssb)
        P.op("dve", lambda e: e.scalar_tensor_tensor(ybuf[:, :], X[xslot][:, :], ssb[:, 3:4], g_fin_b[:, :], ALU.mult, ALU.mult),
             reads=[X[xslot], ssb, g_fin_b], writes=[ybuf])
        out_dma(dst_ap, ybuf, ybuf[:, :])

    import os
    if os.environ.get("KSTOP", "0") == "1":
        P.finish()
        with nc.Block() as block:
            block.tensor(lambda e: P.replay("pe", e)); block.scalar(lambda e: P.replay("act", e)); block.vector(lambda e: P.replay("dve", e))
            block.gpsimd(lambda e: P.replay("pool", e)); block.sync(lambda e: P.replay("sp", e))
        es.close()
        return nc
    dma_in(X[XS], X[XS][:, :], xs.ap())
    ffn([XS], "w1")
    norm_T(X[XS], 0)
    dcp = Buf(None)
    P.op("pool", lambda e: e.dma_start(out=conv_s.ap()[:, 0:HALO - 8, :], in_=state.ap()[:, 8:HALO, :]), dma=dcp, is_out=True)
    in_proj([XS], [XS], 0, cs_s, True)
    pg_ = PB[2]

    def trg(e):
        for cc in range(4):
            ins = e.transpose(pg_[:, cc * 128:(cc + 1) * 128], gtok[:, cc * 128:(cc + 1) * 128], ident[:])
        return ins
    P.op("pe", trg, reads=[gtok, ident], writes=[pg_])
    P.op("act", lambda e: e.copy(gtk2[:, :], pg_[:, :]), reads=[pg_], writes=[gtk2])
    for b in range(16):
        out_dma(conv_s.ap()[b, HALO - 8:HALO, :], gtk2, gtk2[b * 8:(b + 1) * 8, :])
    q_path(0, cs_sT, 0, qabsT_s, qpeT_s)

    pmod = sb("pmod", [128, 1], I32)
    dma_in(pmod, pmod[:], pmod_d.ap())
    ptx = sb("ptx", [128, 16, NG8], I32)
    for i8 in range(8):
        src = bass.AP(tensor=ptab, offset=i8, ap=[[0, 16], [NPG, 16], [8, NG8]])
        P.op("sp", lambda e, i8=i8, src=src: e.dma_start(out=ptx[16 * i8:16 * i8 + 16, :, :], in_=src, allow_slow_non_contiguous=True), writes=[ptx], dma=ptx)
    gidx = sb("gidx", [128, 16, NG8], I32)
    P.op("pool", lambda e: e.tensor_scalar(gidx[:], ptx[:], 16, pmod[:, 0:1], op0=ALU.mult, op1=ALU.add), reads=[ptx, pmod], writes=[gidx])

    units = []
    for b in range(16):
        units.append((b, -1))
        for g in range(NG8):
            units.append((b, g))
    sstate = {"i": 0, "gi": 0, "prev": None, "done": False}

    def qviews(b):
        qa = [qabsT_s.t[:, ch, :].rearrange("p (h q) -> p h q", q=128)[:, :, b * 8:(b + 1) * 8] for ch in range(2)]
        qpv = qpeT_s.t[:, :].rearrange("p (h q) -> p h q", q=128)[:, :, b * 8:(b + 1) * 8]
        return qa, qpv

    def s_gather(b, g):
        cgb, rgb = cg[sstate["gi"] % NCG], rg[sstate["gi"] % NCG]
        sstate["gi"] += 1
        P.op("pool", lambda e: e.indirect_dma_start(
            out=cgb[:, :, :].rearrange("p t c -> p (t c)"), out_offset=None, in_=ckv.ap(),
            in_offset=bass.IndirectOffsetOnAxis(ap=gidx[:, b, g:g + 1], axis=0)), reads=[gidx], writes=[cgb], dma=cgb)
        P.op("pool", lambda e: e.indirect_dma_start(
            out=rgb[:, :, :].rearrange("p t c -> p (t c)"), out_offset=None, in_=ckr.ap(),
            in_offset=bass.IndirectOffsetOnAxis(ap=gidx[:, b, g:g + 1], axis=0)), reads=[gidx], writes=[rgb], dma=rgb)
        return cgb, rgb

    def s_T(cgb, rgb, pr):
        cTb, rTb = cTs[pr % 2], rTs[pr % 2]

        def tr(e):
            for t2 in range(2):
                t = pr * 2 + t2
                e.transpose(SPT[:, t2 * 256:t2 * 256 + 128], cgb[:, t, 0:128], identb[:])
                e.transpose(SPT[:, t2 * 256 + 128:t2 * 256 + 256], cgb[:, t, 128:256], identb[:])
            return e.transpose(SPT[0:64, 512:640], rgb[:, pr * 2:pr * 2 + 2, :].rearrange("p t c -> p (t c)"), identb[:])
        P.op("pe", tr, reads=[cgb, rgb, identb], writes=[SPT])
        if pr % 2 == 0:
            P.op("dve", lambda e: e.tensor_copy(cTb[:, :, :].rearrange("p t c -> p (t c)"), SPT[:, 0:512]), reads=[SPT], writes=[cTb])
            P.op("dve", lambda e: e.tensor_copy(rTb[0:64, :], SPT[0:64, 512:640]), reads=[SPT], writes=[rTb])
        else:
            P.op("act", lambda e: e.copy(cTb[:, :, :].rearrange("p t c -> p (t c)"), SPT[:, 0:512]), reads=[SPT], writes=[cTb])
            P.op("act", lambda e: e.copy(rTb[0:64, :], SPT[0:64, 512:640]), reads=[SPT], writes=[rTb])

    def s_QK(b, pr):
        cTb, rTb = cTs[pr % 2], rTs[pr % 2]
        qa, qpv = qviews(b)

        def qk(e):
            for t2 in range(2):
                t = pr * 2 + t2
                e.matmul(SPST[:, t * 64:(t + 1) * 64], cTb[:, t2, 0:128], qa[0], start=True, stop=False)
                e.matmul(SPST[:, t * 64:(t + 1) * 64], cTb[:, t2, 128:256], qa[1], start=False, stop=False)
                ins = e.matmul(SPST[:, t * 64:(t + 1) * 64], rTb[32 * t2:32 * t2 + 32, :], qpv[32 * t2:32 * t2 + 32], start=False, stop=True)
            return ins
        P.op("pe", qk, reads=[cTb, rTb, qabsT_s, qpeT_s], writes=[SPST])

    def s_PV(pv_info, lo, hi, final):
        b, cgb, pTb, new = pv_info

        def pv(e):
            for t in range(lo, hi):
                v0 = cnat_s[:, 0:128] if new else cgb[:, t, 0:128]
                v1 = cnat_s[:, 128:256] if new else cgb[:, t, 128:256]
                e.matmul(SACC[:, 0:64], v0, pTb[:, t, :], start=False, stop=False, skip_group_check=True)
                e.matmul(SACC[:, 64:128], v1, pTb[:, t, :], start=False, stop=False, skip_group_check=True)
                ins = e.matmul(SACC[:, 128:192], onesb[:, :], pTb[:, t, :], start=False, stop=final, skip_group_check=True)
            return ins
        P.op("pe", pv, reads=[pTb, onesb] + ([cnat_s] if new else [cgb]), writes=[SACC])

    def s_finalize(b):
        P.op("act", lambda e: e.copy(osb[:, :], SACC[:, 0:192]), reads=[SACC], writes=[osb])
        P.op("dve", lambda e: e.reciprocal(osb[:, 128:192], osb[:, 128:192]), reads=[osb], writes=[osb])
        for ch in range(2):
            P.op("dve", lambda e, ch=ch: e.tensor_tensor(
                oTs.t[:, ch, :].rearrange("p (h q) -> p h q", q=128)[:, :, b * 8:(b + 1) * 8],
                osb.t[:, ch * 64:(ch + 1) * 64].rearrange("p (h t) -> p h t", t=8),
                osb.t[:, 128:192].rearrange("p (h t) -> p h t", t=8), ALU.mult), reads=[osb], writes=[oTs])

    def s_zero():
        P.op("pe", lambda e: e.matmul(SACC[:, 0:192], zerob[:, 0:128], zerob[:, 0:192], start=True, stop=True), reads=[zerob], writes=[SACC])

    def s_prev_pv(lo, hi):
        pi = sstate["prev"]
        if pi is None:
            return
        b, cgb, pTb, new, nsub, is_last_of_batch = pi
        lo2, hi2 = min(lo, nsub), min(hi, nsub)
        if hi2 > lo2:
            s_PV((b, cgb, pTb, new), lo2, hi2, final=(is_last_of_batch and hi2 == nsub))
        if hi >= 8:
            if is_last_of_batch:
                s_finalize(b)
                s_zero()
            sstate["prev"] = None

    gath = {}
    gq = {"next": 0}
    gunits = [i for i, (b_, g_) in enumerate(units) if g_ >= 0]

    def ensure_gathers(upto_unit):
        while gq["next"] < len(gunits) and gunits[gq["next"]] <= upto_unit:
            ui = gunits[gq["next"]]
            gq["next"] += 1
            gath[ui] = s_gather(units[ui][0], units[ui][1])

    def stream():
        s_zero()
        ensure_gathers(2)
        for i, (b, g) in enumerate(units):
            sstate["i"] = i + 1
            is_last = (i + 1 == len(units)) or units[i + 1][0] != b
            pTb = pTs[i % 2]
            if g < 0:
                s_prev_pv(0, 8)
                qa, qpv = qviews(b)

                def qk(e, qa=qa, qpv=qpv):
                    e.matmul(SPST[:, 0:64], cT_s[:, 0, :], qa[0], start=True, stop=False)
                    e.matmul(SPST[:, 0:64], cT_s[:, 1, :], qa[1], start=False, stop=False)
                    return e.matmul(SPST[:, 0:64], cTr_s[0:32, :], qpv[0:32], start=False, stop=True)
                P.op("pe", qk, reads=[cT_s, cTr_s, qabsT_s, qpeT_s], writes=[SPST])
                P.op("act", lambda e, pTb=pTb: e.activation(pTb[:, 0, :], SPST[:, 0:64], AF.Exp, scale=SM_SCALE), reads=[SPST], writes=[pTb])
                P.op("dve", lambda e, pTb=pTb, b=b: e.tensor_tensor(pTb[:, 0, :], pTb[:, 0, :], smask[:, b * 64:(b + 1) * 64], ALU.mult), reads=[pTb, smask], writes=[pTb])
                sstate["prev"] = (b, None, pTb, True, 1, is_last)
                yield
                continue
            cgb, rgb = gath.pop(i)
            s_T(cgb, rgb, 0)
            s_prev_pv(0, 4)
            yield
            s_T(cgb, rgb, 1)
            s_QK(b, 0)
            s_prev_pv(4, 8)
            ensure_gathers(i + 3)
            yield
            s_T(cgb, rgb, 2)
            s_QK(b, 1)
            yield
            s_T(cgb, rgb, 3)
            s_QK(b, 2)
            yield
            s_QK(b, 3)
            P.op("act", lambda e, pTb=pTb: e.activation(pTb[:, :, :].rearrange("p t q -> p (t q)"), SPST[:, :], AF.Exp, scale=SM_SCALE), reads=[SPST], writes=[pTb])
            sstate["prev"] = (b, cgb, pTb, False, 8, is_last)
            yield
        s_prev_pv(0, 8)
        sstate["done"] = True

    sgen = stream()

    def pump():
        if sstate["done"]:
            return
        try:
            next(sgen)
        except StopIteration:
            sstate["done"] = True

    def sample_tail():
        while not sstate["done"]:
            pump()
        pa_ = PB[2]

        def aos(e):
            for h in range(NH):
                for ch in range(2):
                    ins = e.matmul(pa_[:, h * 64:(h + 1) * 64], oTs[:, ch, h * 128:(h + 1) * 128], wuv[:, ch, h, :], start=(ch == 0), stop=(ch == 1))
            return ins
        P.op("pe", aos, reads=[oTs, wuv], writes=[pa_])
        P.op("act", lambda e: e.copy(attn[:, :], pa_[:, :]), reads=[pa_], writes=[attn])

        def gin_sample(cc, k):
            return gT_s.t[:, cc, :].rearrange("p (b s) -> p b s", s=38)[:, :, k:k + 8]
        conv_post(XS, gin_sample, gT_s, True)
        ffn([XS], "w2")
        final_out(XS, y_s.ap())

    TAIL_AFTER = max(NGRP - 2, 0)
    import os
    STOP = int(os.environ.get("KSTOP", "0"))
    SEQ = os.environ.get("KSEQ", "0") == "1"
    if STOP == 3 or SEQ:
        pump_ = None
    else:
        pump_ = pump
    if SEQ:
        sample_tail()
    for grp in range(NGRP if STOP not in (1, 2) else 0):
        for i in range(4):
            lt = grp * 4 + i
            dma_in(X[i], X[i][:, :], xp.ap()[lt * 128:(lt + 1) * 128, :])
        ffn([0, 1, 2, 3], "w1", pump_)
        for i in range(4):
            norm_T(X[i], i)
        if grp == 0:
            P.op("pool", lambda e: e.memset(gT[:, :, 0:HALO], 0.0), writes=[gT])
        else:
            P.op("pool", lambda e: e.tensor_copy(gT[:, :, 0:HALO], gT[:, :, 512:512 + HALO]), reads=[gT], writes=[gT])
        in_proj([0, 1, 2, 3], [1, 3], grp * 4, cs_p, False, pump_)
        for i in (1, 3):
            L = grp * 4 + i
            own_idx = L // 2
            q_path(i, cs_pT, own_idx, qabsT, qpeT)
            attend_prompt(L, None)

            def gin_p(cc, k, i=i):
                return gT[:, cc, i * 128 + k:i * 128 + k + 128]
            conv_post(i, gin_p, gT, False)
            if L == NT - 1:
                def trp2(e, i=i):
                    for cc in range(4):
                        ins = e.transpose(PT0[0:HALO, cc * 128:(cc + 1) * 128], gT[:, cc, HALO + i * 128 + 128 - HALO:HALO + i * 128 + 128], identb[:])
                    return ins
                P.op("pe", trp2, reads=[gT, identb], writes=[PT0])
                P.op("act", lambda e: e.copy(gtk2[0:HALO, :], PT0[0:HALO, 0:512]), reads=[PT0], writes=[gtk2])
                out_dma(conv_p.ap(), gtk2, gtk2[0:HALO, :])
        ffn([1, 3], "w2", pump_)
        for i in (1, 3):
            own_idx = (grp * 4 + i) // 2
            final_out(i, y_p.ap()[own_idx * 128:(own_idx + 1) * 128, :])
        if grp == TAIL_AFTER and STOP not in (3, 4) and not SEQ:
            sample_tail()

    P.finish()
    with nc.Block() as block:
        @block.tensor
        def _(e):
            P.replay("pe", e)

        @block.scalar
        def _(e):
            P.replay("act", e)

        @block.vector
        def _(e):
            P.replay("dve", e)

        @block.gpsimd
        def _(e):
            P.replay("pool", e)

        @block.sync
        def _(e):
            P.replay("sp", e)
    print('SBUF bytes remaining', nc.sbuf_bytes_remaining, 'nsem', P.nsem, {k: len(v) for k, v in P.lists.items()}, 'units pumped', sstate["i"], len(units))
    es.close()
    return nc


def rope_tab(pos):
    inv = (10000.0 ** (-np.arange(0, 32, 2, dtype=np.float32) / np.float32(32))).astype(np.float32)
    ang = pos.astype(np.float32)[:, None] * inv[None, :]
    c, s = np.cos(ang).astype(np.float32), np.sin(ang).astype(np.float32)
    return np.concatenate([c, c, s, s], axis=1)


_CACHE = {}
DBG = False
DBG_OUT = {}


def kernel(x_prompt, x_sample, cache_kv_latent, cache_k_rope, state_conv, page_table,
           g_ffn1, w1_gate, w1_up, w1_down, g_mix, w_in, g_q, w_q_b, g_kv, w_kv_b,
           w_dw, b_dw, g_cn, b_cn, g_out_attn, g_out_conv, w_out,
           g_ffn2, w2_gate, w2_up, w2_down, g_final):
    f = lambda a: np.ascontiguousarray(np.asarray(a))
    x_prompt = f(x_prompt); x_sample = f(x_sample)
    B, S, _ = x_prompt.shape
    DB, T, _ = x_sample.shape
    NPHYS = cache_kv_latent.shape[1]
    NPG = page_table.shape[1]
    past = NPG * 128
    NTILE = S // 128
    NT = NTILE
    assert DB == 128 and T == 8 and B == 4
    key = (NT, NPG, NPHYS)
    if key not in _CACHE:
        _CACHE[key] = build(NT, NPG, NPHYS, DBG)
    nc = _CACHE[key]
    ckv = f(cache_kv_latent).reshape(NPHYS * 16, 8 * KVR)
    ckr = f(cache_k_rope).reshape(NPHYS * 16, 8 * RD)
    page_table = f(page_table).astype(np.int32)
    common = dict(
        ckv=ckv, ckr=ckr,
        w1g=f(w1_gate)[0], w1u=f(w1_up)[0], w1d=f(w1_down)[0], w2g=f(w2_gate)[0], w2u=f(w2_up)[0], w2d=f(w2_down)[0],
        w_in=f(w_in)[0], w_qb=f(w_q_b)[0].reshape(QR, NH * 96), w_kvb=f(w_kv_b)[0].reshape(KVR, NH * 128),
        w_dw=f(w_dw)[0], w_out=f(w_out)[0],
        g_ffn1=f(g_ffn1), g_mix=f(g_mix), g_ffn2=f(g_ffn2), g_final=f(g_final).reshape(1, D), g_q=f(g_q), g_kv=f(g_kv),
        b_dw=f(b_dw), g_cn=f(g_cn), b_cn=f(b_cn), g_oa=f(g_out_attn), g_oc=f(g_out_conv),
        ident=np.eye(128, dtype=np.float32),
        tril=np.triu(np.ones((128, 128), np.float32)),
        pmod=(np.arange(128, dtype=np.int32) % 16).reshape(128, 1),
    )
    k = np.arange(128)
    sm = np.zeros((128, 16, NH, 8), np.float32)
    for b in range(16):
        for t in range(8):
            sm[(k // 8 == b) & (k % 8 <= t), b, :, t] = 1.0
    common["smask"] = sm.reshape(128, 16 * 64)
    cs_s = rope_tab(past + np.arange(8))
    cs_s128 = np.tile(cs_s, (16, 1))
    common["cs_s"] = cs_s128
    common["cs_sT"] = np.ascontiguousarray(np.concatenate([cs_s128[:, 0:32].T, cs_s128[:, 32:64].T], axis=0))
    sel = np.zeros((128, 64), np.float32)
    for r_ in range(2):
        sel[64 + np.arange(32), 32 * r_ + np.arange(32)] = 1.0
        sel[96 + np.arange(32), 32 * r_ + np.arange(32)] = 1.0
    common["selsum"] = sel
    in_maps = []
    for c in range(8):
        b, par = c // 2, c % 2
        xb = x_prompt[b].reshape(NTILE, 128, D)
        if par == 0:
            gt = np.arange(-1, NTILE - 1)
        else:
            gt = np.arange(0, NTILE)
        xl = np.zeros((NT, 128, D), np.float32)
        for lt in range(NT):
            if gt[lt] >= 0:
                xl[lt] = xb[gt[lt]]
        pos = (gt[:, None] * 128 + np.arange(128)[None, :]).reshape(-1)
        pos = np.maximum(pos, 0)
        cs = rope_tab(pos)
        own = cs.reshape(NT, 128, 64)[1::2].reshape(-1, 64)
        m = dict(common)
        m["xp"] = xl.reshape(NT * 128, D)
        m["xs"] = x_sample[16 * c:16 * c + 16].reshape(128, D)
        m["ptab"] = page_table[16 * c:16 * c + 16]
        m["state"] = f(state_conv)[0, 16 * c:16 * c + 16]
        m["cs_p"] = cs
        m["cs_pT"] = np.ascontiguousarray(np.concatenate([own[:, 0:32].T, own[:, 32:64].T], axis=0))
        m["m0"] = np.full((128, 1), float(par), np.float32)
        in_maps.append(m)
    res = run_bass_kernel_spmd(nc, in_maps, core_ids=list(range(8)))
    R = res.results
    y_prompt = np.zeros((B, S, D), np.float32)
    new_kv_p = np.zeros((1, B, NTILE, 128, KVR), np.float32)
    new_kr_p = np.zeros((1, B, NTILE, 128, RD), np.float32)
    new_cv_p = np.zeros((1, B, HALO, CD), np.float32)
    y_sample = np.zeros((DB, T, D), np.float32)
    new_kv_s = np.zeros((1, DB, T, KVR), np.float32)
    new_kr_s = np.zeros((1, DB, T, RD), np.float32)
    new_cv_s = np.zeros((1, DB, HALO, CD), np.float32)
    for c in range(8):
        b, par = c // 2, c % 2
        r = R[c]
        gown = np.arange(par, NTILE, 2)
        yp = np.asarray(r["y_p"]).reshape(NT // 2, 128, D)
        kvp = np.asarray(r["kv_p"]).reshape(NT // 2, 128, KVR)
        krp = np.asarray(r["kr_p"]).reshape(NT // 2, 128, RD)
        for j, g in enumerate(gown):
            y_prompt[b, g * 128:(g + 1) * 128] = yp[j]
            new_kv_p[0, b, g] = kvp[j]
            new_kr_p[0, b, g] = krp[j]
        if par == 1:
            new_cv_p[0, b] = np.asarray(r["conv_p"])
        y_sample[16 * c:16 * c + 16] = np.asarray(r["y_s"]).reshape(16, 8, D)
        new_kv_s[0, 16 * c:16 * c + 16] = np.asarray(r["kv_s"]).reshape(16, 8, KVR)
        new_kr_s[0, 16 * c:16 * c + 16] = np.asarray(r["kr_s"]).reshape(16, 8, RD)
        new_cv_s[0, 16 * c:16 * c + 16] = np.asarray(r["conv_s"])
    return (y_prompt, y_sample, new_kv_p, new_kr_p, new_cv_p, new_kv_s, new_kr_s, new_cv_s)
```

```python
import contextlib
import numpy as np
import concourse.bass as bass
import concourse.mybir as mybir
from concourse.bass_utils import run_bass_kernel_spmd

F32, BF16, I32 = mybir.dt.float32, mybir.dt.bfloat16, mybir.dt.int32
AF = mybir.ActivationFunctionType
ALU = mybir.AluOpType
AX = mybir.AxisListType

D = 1024
DFF = 2816
NF = DFF // 128
QR, KVR, RD, CD = 384, 256, 32, 512
NH = 8
EPS = 1e-6
SM_SCALE = 96.0 ** -0.5
CW = 31
HALO = CW - 1


class Buf:
    def __init__(self, t):
        self.t = t
        self.sem = None
        self.cnt = 0
        self.last_w = None
        self.readers = []

    def __getitem__(self, k):
        return self.t[k]


class Prog:
    ENG = ("pe", "act", "dve", "pool", "sp")

    def __init__(self, nc, es):
        self.nc = nc
        self.es = es
        self.lists = {e: [] for e in self.ENG}
        self.count = {e: 0 for e in self.ENG}
        self.esem = {e: es.enter_context(nc.semaphore("es_" + e)) for e in self.ENG}
        self.seen = {e: {} for e in self.ENG}
        self.semobj = {}
        for e in self.ENG:
            self.semobj[id(self.esem[e])] = self.esem[e]
        self.nsem = 0
        self.out_tokens = []

    def bufsem(self, b):
        if b.sem is None:
            b.sem = self.es.enter_context(self.nc.semaphore("bs%d" % self.nsem))
            self.nsem += 1
            self.semobj[id(b.sem)] = b.sem
        return b.sem

    def op(self, eng, fn, reads=(), writes=(), dma=None, is_out=False, extra=()):
        deps = set(extra)
        for b in reads:
            if b.last_w is not None:
                deps.add(b.last_w)
        for b in writes:
            if b.last_w is not None:
                deps.add(b.last_w)
            deps.update(b.readers)
        waits = []
        dmax = {}
        for (sid, val) in deps:
            dmax[sid] = max(dmax.get(sid, 0), val)
        for (sid, val) in dmax.items():
            if eng == "pe" and dma is None and sid == id(self.esem["pe"]):
                continue
            if self.seen[eng].get(sid, 0) >= val:
                continue
            self.seen[eng][sid] = val
            waits.append((self.semobj[sid], val))
        if dma is not None:
            sem = self.bufsem(dma)
            dma.cnt += 16
            tok = (id(sem), dma.cnt)
            inc = 16
        else:
            sem = self.esem[eng]
            self.count[eng] += 1
            tok = (id(sem), self.count[eng])
            inc = 1
        self.lists[eng].append((waits, fn, sem, inc))
        for b in writes:
            b.last_w = tok
            b.readers = []
        for b in reads:
            b.readers.append(tok)
        if is_out:
            self.out_tokens.append(tok)
        return tok

    def finish(self):
        need = {}
        for (sid, val) in self.out_tokens:
            need[sid] = max(need.get(sid, 0), val)
        waits = [(self.semobj[s], v) for s, v in need.items()]
        self.lists["sp"].append((waits, None, None, 0))

    def replay(self, eng, e):
        for waits, fn, sem, inc in self.lists[eng]:
            for (s, v) in waits:
                e.wait_ge(s, v)
            if fn is None:
                continue
            ins = fn(e)
            ins.then_inc(sem, inc)


def build(NT, NPG, NPHYS, dbg=False):
    NOWN = NT // 2
    NGRP = NT // 4
    NG8 = NPG // 8
    nc = bass.Bass("TRN2", target_bir_lowering=False)
    es = contextlib.ExitStack()

    def din(name, shape, dt=F32):
        return nc.dram_tensor(name, list(shape), dt, kind="ExternalInput")

    def dout(name, shape, dt=F32):
        return nc.dram_tensor(name, list(shape), dt, kind="ExternalOutput")

    xp = din("xp", [NT * 128, D])
    xs = din("xs", [128, D])
    ckv = din("ckv", [NPHYS * 16, 8 * KVR])
    ckr = din("ckr", [NPHYS * 16, 8 * RD])
    ptab = din("ptab", [16, NPG], I32)
    state = din("state", [16, HALO, CD])
    WD = {}
    for nm, shp in (("w1g", [D, DFF]), ("w1u", [D, DFF]), ("w1d", [DFF, D]), ("w2g", [D, DFF]), ("w2u", [D, DFF]), ("w2d", [DFF, D]),
                    ("w_in", [D, 1696]), ("w_out", [D, D])):
        WD[nm] = din(nm, shp)
    w_qb = din("w_qb", [QR, NH * 96])
    w_kvb = din("w_kvb", [KVR, NH * 128])
    w_dw = din("w_dw", [CW, CD])
    gvec = {}
    for nm, n in (("g_ffn1", D), ("g_mix", D), ("g_ffn2", D), ("g_final", D), ("g_q", QR), ("g_kv", KVR),
                  ("b_dw", CD), ("g_cn", CD), ("b_cn", CD), ("g_oa", CD), ("g_oc", CD)):
        gvec[nm] = din(nm, [1, n])
    cs_p = din("cs_p", [NT * 128, 64])
    cs_pT = din("cs_pT", [64, NOWN * 128])
    selsum_d = din("selsum", [128, 64])
    cs_s = din("cs_s", [128, 64])
    cs_sT = din("cs_sT", [64, 128])
    ident_d = din("ident", [128, 128])
    tril_d = din("tril", [128, 128])
    m0_d = din("m0", [128, 1])
    pmod_d = din("pmod", [128, 1], I32)
    smask_d = din("smask", [128, 16 * 64])

    y_p = dout("y_p", [NOWN * 128, D]); y_s = dout("y_s", [128, D])
    kv_p = dout("kv_p", [NOWN * 128, KVR]); kr_p = dout("kr_p", [NOWN * 128, RD])
    conv_p = dout("conv_p", [HALO, CD])
    kv_s = dout("kv_s", [128, KVR]); kr_s = dout("kr_s", [128, RD])
    conv_s = dout("conv_s", [16, HALO, CD])

    P = Prog(nc, es)

    def sb(name, shape, dt=F32):
        return Buf(es.enter_context(nc.sbuf_tensor("s_" + name, list(shape), dt)))

    def ps(name, shape, dt=F32):
        return Buf(es.enter_context(nc.psum_tensor("p_" + name, list(shape), dt)))

    def alias(parent, view):
        b = Buf(view)
        b.last_w = parent.last_w
        b.readers = list(parent.readers)
        return b

    PB = [ps("pb%d" % i, [128, 512]) for i in range(4)]
    PT0 = ps("pt0", [128, 1024], BF16)
    SPST = ps("spst", [128, 512])
    SPT = ps("spt", [128, 1024], BF16)
    SACC = ps("sacc", [128, 512])

    ident = sb("ident", [128, 128]); identb = sb("identb", [128, 128], BF16)
    tril = sb("tril", [128, 128]); trilb = sb("trilb", [128, 4, 128], BF16)
    m0 = sb("m0", [128, 1])
    onesb = sb("onesb", [128, 128], BF16)
    zerob = sb("zerob", [128, 512], BF16)
    onesf = sb("onesf", [128, 128])
    selsum = sb("selsum", [128, 64])

    def dma_in(dst, dst_ap, src_ap, eng="sp"):
        P.op(eng, lambda e, o=dst_ap, i=src_ap: e.dma_start(out=o, in_=i), writes=[dst], dma=dst)

    def out_dma(dst_ap, src, src_ap):
        P.op("pool", lambda e: e.dma_start(out=dst_ap, in_=src_ap), reads=[src], dma=src, is_out=True)

    dma_in(ident, ident[:], ident_d.ap())
    dma_in(tril, tril[:], tril_d.ap())
    dma_in(m0, m0[:], m0_d.ap())
    dma_in(selsum, selsum[:], selsum_d.ap())
    P.op("dve", lambda e: e.tensor_copy(identb[:], ident[:]), reads=[ident], writes=[identb])
    for h in range(4):
        P.op("dve", lambda e, h=h: e.tensor_copy(trilb[:, h, :], tril[:]), reads=[tril], writes=[trilb])
    P.op("pool", lambda e: e.memset(onesb[:], 1.0), writes=[onesb])
    P.op("pool", lambda e: e.memset(zerob[:], 0.0), writes=[zerob])
    P.op("pool", lambda e: e.memset(onesf[:], 1.0), writes=[onesf])

    def bvec(nm, n):
        b = sb("bv_" + nm, [128, n])
        dma_in(b, b[:], gvec[nm].ap()[0:1, :].partition_broadcast(128), eng="pool")
        return b
    g_kv_b = bvec("g_kv", KVR); g_cn_b = bvec("g_cn", CD); b_cn_b = bvec("b_cn", CD)
    g_fin_b = bvec("g_final", D)

    def cvec(nm, n):
        b = sb("cv_" + nm, [128, n // 128])
        P.op("sp", lambda e: e.dma_start(out=b[:], in_=gvec[nm].ap().rearrange("o (c p) -> p (o c)", p=128),
                                         allow_slow_non_contiguous=True), writes=[b], dma=b)
        return b
    gc = {nm: cvec(nm, n) for nm, n in (("g_ffn1", D), ("g_mix", D), ("g_ffn2", D), ("g_q", QR), ("b_dw", CD),
                                        ("g_oa", CD), ("g_oc", CD))}
    wdw = sb("wdw", [128, 4, CW])
    for c in range(4):
        P.op("sp", lambda e, c=c: e.dma_start(out=wdw[:, c, :], in_=w_dw.ap()[:, c * 128:(c + 1) * 128].rearrange("k p -> p k"),
                                              allow_slow_non_contiguous=True), writes=[wdw], dma=wdw)

    NSTG, NWB = 4, 8
    stg = [sb("stg%d" % i, [128, 1024]) for i in range(NSTG)]
    wbf = [sb("wbf%d" % i, [128, 1024], BF16) for i in range(NWB)]
    gxt = sb("gxt", [128, 8, 128])
    rr = {"stg": 0, "wb": 0, "cast": 0}
    NPIECE = 6 * NF + 16 + 8
    scr = nc.dram_tensor("scr", [NPIECE, 128, 1024], BF16, kind="Internal")
    piece_idx = {}
    piece_tok = {}

    def set_gx(col):
        for c in range(8):
            P.op("pool", lambda e, c=c: e.tensor_scalar(gxt[:, c, :], onesf[:], col[:, c:c + 1], None, op0=ALU.mult),
                 reads=[col, onesf], writes=[gxt])

    def _to_scratch(key, w):
        idx = len(piece_idx)
        piece_idx[key] = idx
        piece_tok[key] = P.op("act", lambda e: e.dma_start(out=scr.ap()[idx], in_=w[:, :]), reads=[w], dma=w)

    specs = []
    for f in range(NF):
        specs += [("col", "w1g", f * 128, 128, "g_ffn1"), ("col", "w1u", f * 128, 128, "g_ffn1"), ("row", "w1d", f * 128, None, None)]
    for c0 in (0, 128, 256, 384, 512):
        specs.append(("col", "w_in", c0, 128, "g_mix"))
    specs.append(("col", "w_in", 640, 32, "g_mix"))
    for cc in range(8):
        specs.append(("col", "w_in", 672 + cc * 128, 128, "g_mix"))
    for k in range(8):
        specs.append(("row", "w_out", k * 128, None, "g_oa" if k < 4 else "g_oc"))
    for f in range(NF):
        specs += [("col", "w2g", f * 128, 128, "g_ffn2"), ("col", "w2u", f * 128, 128, "g_ffn2"), ("row", "w2d", f * 128, None, None)]
    LOOK = 3
    loaded = {}

    def prep_load(j):
        kind, Wn, off, ncols, gname = specs[j]
        W = WD[Wn]
        s_ = stg[rr["stg"] % NSTG]; rr["stg"] += 1
        if kind == "col":
            sv = s_.t[:].rearrange("p (c j) -> p c j", j=128)
            src = W.ap()[:, off:off + ncols].rearrange("(c p) j -> p c j", p=128)
            P.op("sp", lambda e: e.dma_start(out=sv[:, :, 0:ncols], in_=src), writes=[s_], dma=s_)
        else:
            P.op("sp", lambda e: e.dma_start(out=s_[:, :], in_=W.ap()[off:off + 128, :]), writes=[s_], dma=s_)
        loaded[j] = s_

    cur_gx = {"name": None}

    def prep_cast_store(j):
        kind, Wn, off, ncols, gname = specs[j]
        s_ = loaded.pop(j)
        w = wbf[rr["wb"] % NWB]; rr["wb"] += 1
        if kind == "col":
            if cur_gx["name"] != gname:
                set_gx(gc[gname])
                cur_gx["name"] = gname
            sv = s_.t[:].rearrange("p (c j) -> p c j", j=128)
            wv = w.t[:].rearrange("p (c j) -> p c j", j=128)
            rr["cast"] += 1
            ce = "pool" if rr["cast"] % 3 == 0 else "dve"
            P.op(ce, lambda e: e.tensor_tensor(wv[:, :, 0:ncols], sv[:, :, 0:ncols], gxt[:, :, 0:ncols], ALU.mult), reads=[s_, gxt], writes=[w])
        elif gname is not None:
            k = off // 128
            sb_ = gc[gname]
            scol = sb_[:, (k % 4):(k % 4) + 1]
            P.op("act", lambda e: e.activation(w[:, :], s_[:, :], AF.Copy, scale=scol), reads=[s_, sb_], writes=[w])
        else:
            P.op("act", lambda e: e.copy(w[:, :], s_[:, :]), reads=[s_], writes=[w])
        _to_scratch((Wn, off), w)

    for j in range(min(LOOK, len(specs))):
        prep_load(j)
    for i in range(len(specs)):
        prep_cast_store(i)
        if i + LOOK < len(specs):
            prep_load(i + LOOK)

    def piece(Wn, off, col=False):
        w = wbf[rr["wb"] % NWB]; rr["wb"] += 1
        idx = piece_idx[(Wn, off)]
        P.op("sp", lambda e: e.dma_start(out=w[:, :], in_=scr.ap()[idx]), writes=[w], dma=w, extra=[piece_tok[(Wn, off)]])
        if col:
            return w, w.t[:].rearrange("p (c j) -> p c j", j=128)
        return w

    wq = sb("wq", [128, 3, NH, 128], BF16)
    for c in range(3):
        s_ = stg[rr["stg"] % NSTG]; rr["stg"] += 1
        P.op("sp", lambda e, c=c, s_=s_: e.dma_start(out=s_[:, 0:NH * 96], in_=w_qb.ap()[c * 128:(c + 1) * 128, :]), writes=[s_], dma=s_)
        sv = s_.t[:, 0:NH * 96].rearrange("p (h j) -> p h j", j=96)
        sc = gc["g_q"][:, c:c + 1]
        P.op("dve", lambda e, c=c, sc=sc, sv=sv: e.tensor_scalar(wq[:, c, :, 0:96], sv[:, :, :], sc, None, op0=ALU.mult),
             reads=[s_, gc["g_q"]], writes=[wq])
        P.op("dve", lambda e, c=c, sc=sc, sv=sv: e.tensor_scalar(wq[:, c, :, 96:112], sv[:, :, 80:96], sc, -1.0, op0=ALU.mult, op1=ALU.mult),
             reads=[s_, gc["g_q"]], writes=[wq])
        P.op("dve", lambda e, c=c, sc=sc, sv=sv: e.tensor_scalar(wq[:, c, :, 112:128], sv[:, :, 64:80], sc, None, op0=ALU.mult),
             reads=[s_, gc["g_q"]], writes=[wq])
    wuv = sb("wuv", [128, 2, NH, 64], BF16)
    wukT = sb("wukT", [64, NH, 256], BF16)
    for c in range(2):
        s_ = stg[rr["stg"] % NSTG]; rr["stg"] += 1
        P.op("sp", lambda e, c=c, s_=s_: e.dma_start(out=s_[:, :], in_=w_kvb.ap()[c * 128:(c + 1) * 128, :]), writes=[s_], dma=s_)
        sv = s_.t[:, :].rearrange("p (h j) -> p h j", j=128)
        P.op("dve", lambda e, c=c, sv=sv: e.tensor_copy(wuv[:, c, :, :], sv[:, :, 64:128]), reads=[s_], writes=[wuv])
        for h in range(NH):
            pb = PB[h % 2]
            P.op("pe", lambda e, h=h, pb=pb, sv=sv: e.transpose(pb[0:64, 0:128], sv[:, h, 0:64], ident[:]),
                 reads=[s_, ident], writes=[pb])
            P.op("act", lambda e, h=h, c=c, pb=pb: e.copy(wukT[:, h, c * 128:(c + 1) * 128], pb[0:64, 0:128]),
                 reads=[pb], writes=[wukT])
    gT_s = sb("gT_s", [128, 4, 16 * 38], BF16)
    for g4 in range(4):
        s_ = stg[rr["stg"] % NSTG]; rr["stg"] += 1
        P.op("sp", lambda e, g4=g4, s_=s_: e.dma_start(out=s_[0:120, 0:CD], in_=state.ap()[g4 * 4:(g4 + 1) * 4].rearrange("b s c -> (b s) c")), writes=[s_], dma=s_)
        for cc in range(4):
            pb = PB[(g4 * 4 + cc) % 2]
            P.op("pe", lambda e, s_=s_, cc=cc, pb=pb: e.transpose(pb[:, 0:120], s_[0:120, cc * 128:(cc + 1) * 128], ident[0:120, 0:120]),
                 reads=[s_, ident], writes=[pb])
            P.op("act", lambda e, g4=g4, cc=cc, pb=pb: e.copy(
                gT_s.t[:, cc, g4 * 4 * 38:(g4 + 1) * 4 * 38].rearrange("p (b s) -> p b s", s=38)[:, :, 0:HALO],
                pb.t[:, 0:120].rearrange("p (b s) -> p b s", s=HALO)), reads=[pb], writes=[gT_s])
    smask = sb("smask", [128, 16 * 64], BF16)
    s_m = stg[rr["stg"] % NSTG]; rr["stg"] += 1
    dma_in(s_m, s_m[:, :], smask_d.ap())
    P.op("dve", lambda e: e.tensor_copy(smask[:], s_m[:, :]), reads=[s_m], writes=[smask])

    NCG = 4
    cg = [alias(stg[i], stg[i].t[:, :].bitcast(BF16).rearrange("p (t c) -> p t c", c=KVR)) for i in range(NCG)]
    rg = [sb("rg%d" % i, [128, 8, RD], BF16) for i in range(NCG)]
    cTs = [sb("cTs%d" % i, [128, 2, 256], BF16) for i in range(2)]
    rTs = [sb("rTs%d" % i, [128, 128], BF16) for i in range(2)]
    pTs = [sb("pTs%d" % i, [128, 8, 64], BF16) for i in range(2)]
    junk = sb("junk", [128, D], BF16)

    TMAX = 4
    X = [sb("X%d" % i, [128, D]) for i in range(TMAX + 1)]
    XS = 4
    xnT = sb("xnT", [128, 8, TMAX * 128], BF16)
    xnb = sb("xnb", [128, D], BF16)
    st_ss = [sb("ss%d" % i, [128, 4]) for i in range(4)]
    rrn = {"n": 0}
    sg = [sb("sg%d" % i, [128, 512]) for i in range(2)]
    hm = [sb("hm%d" % i, [128, 2, 512], BF16) for i in range(2)]
    sgl = sg[0]
    ybuf = sb("ybuf", [128, D])

    def rstd_of(src_ap, n, reads, ssb):
        P.op("act", lambda e: e.activation(junk[:, 0:n], src_ap, AF.Square, accum_out=ssb[:, 0:1]), reads=reads, writes=[junk, ssb])
        P.op("dve", lambda e: e.tensor_scalar(ssb[:, 1:2], ssb[:, 0:1], 1.0 / n, EPS, op0=ALU.mult, op1=ALU.add), reads=[ssb], writes=[ssb])
        P.op("act", lambda e: e.activation(ssb[:, 2:3], ssb[:, 1:2], AF.Sqrt), reads=[ssb], writes=[ssb])
        P.op("dve", lambda e: e.reciprocal(ssb[:, 3:4], ssb[:, 2:3]), reads=[ssb], writes=[ssb])

    def next_ss():
        ssb = st_ss[rrn["n"] % 4]; rrn["n"] += 1
        return ssb

    def norm_T(xb, slot):
        ssb = next_ss()
        rstd_of(xb[:, :], D, [xb], ssb)
        P.op("act", lambda e: e.activation(xnb[:, :], xb[:, :], AF.Copy, scale=ssb[:, 3:4]), reads=[xb, ssb], writes=[xnb])

        def tr(e):
            for c in range(8):
                ins = e.transpose(PT0[:, c * 128:(c + 1) * 128], xnb[:, c * 128:(c + 1) * 128], identb[:])
            return ins
        P.op("pe", tr, reads=[xnb, identb], writes=[PT0])
        P.op("act", lambda e: e.copy(xnT[:, :, slot * 128:(slot + 1) * 128], PT0.t[:].rearrange("p (c j) -> p c j", j=128)),
             reads=[PT0], writes=[xnT])

    def ffn(tiles, wk, pump=None):
        T = len(tiles)
        NTOK = T * 128
        for i, t in enumerate(tiles):
            norm_T(X[t], i)
        for fp in range(NF // 2):
            hb = hm[fp % 2]
            wds = []
            for k in range(2):
                f = fp * 2 + k
                wgb, wgv = piece(wk + "g", f * 128, True)
                wub, wuv_ = piece(wk + "u", f * 128, True)
                wds.append(piece(wk + "d", f * 128))
                pg, pu = PB[0], PB[1]

                def mm(e, wv=wgv, pp=pg):
                    for c in range(8):
                        ins = e.matmul(pp[:, 0:NTOK], wv[:, c, :], xnT[:, c, 0:NTOK], start=(c == 0), stop=(c == 7))
                    return ins
                P.op("pe", mm, reads=[wgb, xnT], writes=[pg])
                if pump is not None:
                    pump()

                def mm2(e, wv=wuv_, pp=pu):
                    for c in range(8):
                        ins = e.matmul(pp[:, 0:NTOK], wv[:, c, :], xnT[:, c, 0:NTOK], start=(c == 0), stop=(c == 7))
                    return ins
                P.op("pe", mm2, reads=[wub, xnT], writes=[pu])
                if pump is not None:
                    pump()
                sgb = sg[f % 2]
                P.op("act", lambda e, sgb=sgb: e.activation(sgb[:, 0:NTOK], pg[:, 0:NTOK], AF.Silu), reads=[pg], writes=[sgb])
                P.op("dve", lambda e, sgb=sgb, hb=hb, k=k: e.tensor_tensor(hb[:, k, 0:NTOK], sgb[:, 0:NTOK], pu[:, 0:NTOK], ALU.mult),
                     reads=[sgb, pu], writes=[hb])
                if pump is not None:
                    pump()
            for i, t in enumerate(tiles):
                def mmd(e, i=i, hb=hb, wds=wds):
                    for dh in range(2):
                        for k in range(2):
                            ins = e.matmul(PB[2 + dh][:, :], hb[:, k, i * 128:(i + 1) * 128], wds[k][:, dh * 512:(dh + 1) * 512],
                                           start=(k == 0), stop=(k == 1))
                    return ins
                P.op("pe", mmd, reads=[hb] + wds, writes=[PB[2], PB[3]])
                for dh in range(2):
                    P.op("dve", lambda e, t=t, dh=dh: e.scalar_tensor_tensor(
                        X[t][:, dh * 512:(dh + 1) * 512], PB[2 + dh][:, :], 0.5, X[t][:, dh * 512:(dh + 1) * 512], ALU.mult, ALU.add),
                        reads=[PB[2 + dh], X[t]], writes=[X[t]])
                if pump is not None:
                    pump()

    NKT = NT
    NRC = (NKT + 2) // 3
    cT = sb("cT", [128, 2, NKT * 128], BF16)
    cTr = sb("cTr", [96, NRC * 128], BF16)
    cnat = sb("cnat", [128, NKT, 256], BF16)
    cT_s = sb("cT_s", [128, 2, 128], BF16); cTr_s = sb("cTr_s", [32, 128], BF16); cnat_s = sb("cnat_s", [128, 256], BF16)
    gT = sb("gT", [128, 4, HALO + TMAX * 128], BF16)
    ukv = sb("ukv", [128, 288])
    ckvf = sb("ckvf", [128, KVR])
    ckvb = sb("ckvb", [128, 288], BF16)
    kpe = sb("kpe", [128, RD])
    cstab = sb("cstab", [128, 64])
    tmp32 = sb("tmp32", [128, 32])
    qcn = sb("qcn", [128, QR]); qcnb = sb("qcnb", [128, QR], BF16)
    qcnT = sb("qcnT", [128, 3, 128], BF16)
    qh = sb("qh", [128, NH, 128])
    qnb = sb("qnb", [64, NH, 128], BF16)
    qabsT = sb("qabsT", [128, 2, NH * 128], BF16)
    qpeT = sb("qpeT", [96, NH * 128], BF16)
    qabsT_s = sb("qabsT_s", [128, 2, NH * 128], BF16)
    qpeT_s = sb("qpeT_s", [96, NH * 128], BF16)
    cs128 = sb("cs128", [128, 128])
    pT = [sb("pT%d" % i, [128, 512], BF16) for i in range(2)]
    oTb = sb("oTb", [128, 2, 512], BF16)
    oTs = sb("oTs", [128, 2, NH * 128], BF16)
    osb = sb("osb", [128, 192])
    lacc = sb("lacc", [128, 8])
    attn = sb("attn", [128, 512])
    yTs = [sb("yT%d" % i, [128, 128]) for i in range(4)]
    ydw = sb("ydw", [128, CD])
    cvo = sb("cvo", [128, CD])
    mix = xnb
    mixT = sb("mixT", [128, 8, 128], BF16)
    stat = sb("stat", [128, 8])
    gtok = ydw
    gtk2 = cvo

    def in_proj(slots, own_slots, kv_tile0, cs_src, is_sample):
        n = len(slots)
        NTOK = n * 128
        kvp = [piece("w_in", c0, True) for c0 in (384, 512, 640)]
        for i, s in enumerate(slots):
            pk = PB[2 + i % 2]

            def mm(e, i=i, pk=pk):
                for j, (wb_, wv) in enumerate(kvp):
                    ncols = 128 if j < 2 else 32
                    for c in range(8):
                        ins = e.matmul(pk[:, j * 128:j * 128 + ncols], xnT[:, c, i * 128:(i + 1) * 128], wv[:, c, 0:ncols],
                                       start=(c == 0), stop=(c == 7))
                return ins
            P.op("pe", mm, reads=[xnT] + [w for w, _ in kvp], writes=[pk])
            P.op("act", lambda e, pk=pk: e.copy(ukv[:, 0:288], pk[:, 0:288]), reads=[pk], writes=[ukv])
            ssb = next_ss()
            rstd_of(ukv[:, 0:KVR], KVR, [ukv], ssb)
            P.op("dve", lambda e, ssb=ssb: e.scalar_tensor_tensor(ckvf[:, :], ukv[:, 0:KVR], ssb[:, 3:4], g_kv_b[:, :], ALU.mult, ALU.mult),
                 reads=[ukv, ssb, g_kv_b], writes=[ckvf])
            tok0 = (kv_tile0 + i) * 128 if not is_sample else 0
            dma_in(cstab, cstab[:], cs_src.ap()[tok0:tok0 + 128, :])
            P.op("dve", lambda e: e.tensor_tensor(kpe[:, :], ukv[:, 256:288], cstab[:, 0:32], ALU.mult), reads=[ukv, cstab], writes=[kpe])
            P.op("dve", lambda e: e.tensor_tensor(tmp32[:, 0:16], ukv[:, 272:288], cstab[:, 32:48], ALU.mult), reads=[ukv, cstab], writes=[tmp32])
            P.op("dve", lambda e: e.tensor_tensor(tmp32[:, 16:32], ukv[:, 256:272], cstab[:, 48:64], ALU.mult), reads=[ukv, cstab], writes=[tmp32])
            P.op("dve", lambda e: e.tensor_sub(kpe[:, 0:16], kpe[:, 0:16], tmp32[:, 0:16]), reads=[tmp32, kpe], writes=[kpe])
            P.op("dve", lambda e: e.tensor_add(kpe[:, 16:32], kpe[:, 16:32], tmp32[:, 16:32]), reads=[tmp32, kpe], writes=[kpe])
            if is_sample:
                out_dma(kv_s.ap(), ckvf, ckvf[:, :])
                out_dma(kr_s.ap(), kpe, kpe[:, :])
            elif s in own_slots:
                ot = (kv_tile0 + i) // 2
                out_dma(kv_p.ap()[ot * 128:(ot + 1) * 128, :], ckvf, ckvf[:, :])
                out_dma(kr_p.ap()[ot * 128:(ot + 1) * 128, :], kpe, kpe[:, :])
            kt = kv_tile0 + i
            P.op("pool", lambda e: e.tensor_copy(ckvb[:, 0:256], ckvf[:, :]), reads=[ckvf], writes=[ckvb])
            P.op("pool", lambda e: e.tensor_copy(ckvb[:, 256:288], kpe[:, :]), reads=[kpe], writes=[ckvb])
            if is_sample:
                d_nat, d_natap = cnat_s, cnat_s[:, :]
                d_T, d_Tap = cT_s, cT_s[:, :, :]
                d_r, d_rap = cTr_s, cTr_s[0:32, :]
            else:
                d_nat, d_natap = cnat, cnat[:, kt, :]
                d_T, d_Tap = cT, cT[:, :, kt * 128:(kt + 1) * 128]
                d_r, d_rap = cTr, cTr[32 * (kt % 3):32 * (kt % 3) + 32, (kt // 3) * 128:(kt // 3 + 1) * 128]
            P.op("pool", lambda e, d=d_natap: e.tensor_copy(d, ckvb[:, 0:256]), reads=[ckvb], writes=[d_nat])
            rb = 32 * (kt % 3) if not is_sample else 0

            def tr(e, rb=rb):
                e.transpose(PT0[:, 0:128], ckvb[:, 0:128], identb[:])
                e.transpose(PT0[:, 128:256], ckvb[:, 128:256], identb[:])
                return e.transpose(PT0[rb:rb + 32, 256:384], ckvb[:, 256:288], identb[:])
            P.op("pe", tr, reads=[ckvb, identb], writes=[PT0])
            P.op("act", lambda e, d=d_Tap: e.copy(d, PT0.t[:, 0:256].rearrange("p (c j) -> p c j", j=128)), reads=[PT0], writes=[d_T])
            P.op("act", lambda e, d=d_rap, rb=rb: e.copy(d, PT0[rb:rb + 32, 256:384]), reads=[PT0], writes=[d_r])
        gdst = gT_s if is_sample else gT
        for cc in range(4):
            wa, wav = piece("w_in", 672 + cc * 128, True)
            wb_, wbv = piece("w_in", 1184 + cc * 128, True)
            pa, pbk = PB[0 + (cc % 2) * 2], PB[1 + (cc % 2) * 2]

            def mma(e, wv=wav, pp=pa):
                for c in range(8):
                    ins = e.matmul(pp[:, 0:NTOK], wv[:, c, :], xnT[:, c, 0:NTOK], start=(c == 0), stop=(c == 7))
                return ins
            P.op("pe", mma, reads=[wa, xnT], writes=[pa])

            def mmb(e, wv=wbv, pp=pbk):
                for c in range(8):
                    ins = e.matmul(pp[:, 0:NTOK], wv[:, c, :], xnT[:, c, 0:NTOK], start=(c == 0), stop=(c == 7))
                return ins
            P.op("pe", mmb, reads=[wb_, xnT], writes=[pbk])
            P.op("act", lambda e, pbk=pbk: e.activation(sgl[:, 0:NTOK], pbk[:, 0:NTOK], AF.Sigmoid), reads=[pbk], writes=[sgl])
            if is_sample:
                P.op("dve", lambda e, cc=cc, pa=pa: e.tensor_tensor(
                    gT_s.t[:, cc, :].rearrange("p (b s) -> p b s", s=38)[:, :, HALO:38],
                    sgl.t[:, 0:128].rearrange("p (b t) -> p b t", t=8), pa.t[:, 0:128].rearrange("p (b t) -> p b t", t=8), ALU.mult),
                    reads=[sgl, pa], writes=[gT_s])
                P.op("dve", lambda e, cc=cc, pa=pa: e.tensor_tensor(gtok[:, cc * 128:(cc + 1) * 128], sgl[:, 0:128], pa[:, 0:128], ALU.mult),
                     reads=[sgl, pa], writes=[gtok])
            else:
                P.op("dve", lambda e, cc=cc, pa=pa: e.tensor_tensor(gT[:, cc, HALO:HALO + NTOK], sgl[:, 0:NTOK], pa[:, 0:NTOK], ALU.mult),
                     reads=[sgl, pa], writes=[gT])

    def q_path(pos, csT_src, own_idx, qa_dst, qp_dst):
        qp = [piece("w_in", c0, True) for c0 in (0, 128, 256)]
        pq = PB[0]

        def mm(e):
            for j, (wb_, wv) in enumerate(qp):
                for c in range(8):
                    ins = e.matmul(pq[:, j * 128:(j + 1) * 128], xnT[:, c, pos * 128:(pos + 1) * 128], wv[:, c, :], start=(c == 0), stop=(c == 7))
            return ins
        P.op("pe", mm, reads=[xnT] + [w for w, _ in qp], writes=[pq])
        P.op("act", lambda e: e.copy(qcn[:, :], pq[:, 0:QR]), reads=[pq], writes=[qcn])
        ssb = next_ss()
        rstd_of(qcn[:, :], QR, [qcn], ssb)
        P.op("dve", lambda e: e.tensor_scalar(qcnb[:, :], qcn[:, :], ssb[:, 3:4], None, op0=ALU.mult), reads=[qcn, ssb], writes=[qcnb])

        def tr(e):
            for c in range(3):
                ins = e.transpose(PT0[:, c * 128:(c + 1) * 128], qcnb[:, c * 128:(c + 1) * 128], identb[:])
            return ins
        P.op("pe", tr, reads=[qcnb, identb], writes=[PT0])
        P.op("act", lambda e: e.copy(qcnT[:, :, :], PT0.t[:, 0:384].rearrange("p (c j) -> p c j", j=128)), reads=[PT0], writes=[qcnT])
        dma_in(cs128, cs128[64:128, :], csT_src.ap()[:, own_idx * 128:(own_idx + 1) * 128])
        for hh in range(2):
            pb = PB[2 + hh]

            def mmq(e, hh=hh, pb=pb):
                for h4 in range(4):
                    h = hh * 4 + h4
                    for c in range(3):
                        ins = e.matmul(pb[:, h4 * 128:(h4 + 1) * 128], wq[:, c, h, :], qcnT[:, c, :], start=(c == 0), stop=(c == 2))
                return ins
            P.op("pe", mmq, reads=[wq, qcnT], writes=[pb])
            P.op("act", lambda e, hh=hh, pb=pb: e.copy(qh[:, hh * 4:(hh + 1) * 4, :], pb.t[:].rearrange("p (h j) -> p h j", j=128)),
                 reads=[pb], writes=[qh])
        P.op("dve", lambda e: e.tensor_copy(qnb[:, :, :], qh[0:64, :, :]), reads=[qh], writes=[qnb])
        for h in range(NH):
            P.op("dve", lambda e, h=h: e.tensor_tensor(qh[64:128, h, :], qh[64:128, h, :], cs128[64:128, :], ALU.mult),
                 reads=[qh, cs128], writes=[qh])
        for hh in range(2):
            pr = PB[2 + hh]
            P.op("pe", lambda e, hh=hh, pr=pr: e.matmul(pr[0:96, 0:512], selsum3[64:128, :],
                                                       qh.t[64:128, hh * 4:(hh + 1) * 4, :].rearrange("p h j -> p (h j)"), start=True, stop=True),
                 reads=[qh, selsum3], writes=[pr])
            P.op("act", lambda e, hh=hh, pr=pr: e.copy(qp_dst[0:96, hh * 512:(hh + 1) * 512], pr[0:96, 0:512]), reads=[pr], writes=[qp_dst])
        for ch in range(2):
            for hh in range(2):
                pb = PB[(ch * 2 + hh) % 2]

                def mma(e, ch=ch, hh=hh, pb=pb):
                    for h4 in range(4):
                        h = hh * 4 + h4
                        ins = e.matmul(pb[:, h4 * 128:(h4 + 1) * 128], wukT[:, h, ch * 128:(ch + 1) * 128], qnb[:, h, :], start=True, stop=True)
                    return ins
                P.op("pe", mma, reads=[wukT, qnb], writes=[pb])
                P.op("act", lambda e, ch=ch, hh=hh, pb=pb: e.copy(qa_dst[:, ch, hh * 512:(hh + 1) * 512], pb[:, :]), reads=[pb], writes=[qa_dst])

    selsum3 = sb("selsum3", [128, 96])
    for r in range(3):
        P.op("dve", lambda e, r=r: e.tensor_copy(selsum3[:, 32 * r:32 * r + 32], selsum[:, 0:32]), reads=[selsum], writes=[selsum3])

    def attend_prompt(L, pump=None):
        for hh in range(2):
            attend_half(L, hh, pump)

    def attend_half(L, hh, pump):
        po = [PB[0], PB[1]]
        plpa = PB[3]

        def z(e):
            e.matmul(po[0][:, :], zerob[:, 0:128], zerob[:, 0:512], start=True, stop=True)
            e.matmul(po[1][:, :], zerob[:, 0:128], zerob[:, 0:512], start=True, stop=True)
            return e.matmul(plpa[:, 0:8], zerob[:, 0:128], zerob[:, 0:8], start=True, stop=True)
        P.op("pe", z, reads=[zerob], writes=[po[0], po[1], plpa])
        for kt in range(L + 2):
            if kt <= L:
                attend_qk(L, hh, kt)
            if kt >= 1:
                attend_pv(L, hh, kt - 1, po, plpa)
            if pump is not None and kt % 2 == 1:
                pump()
        P.op("act", lambda e: e.copy(oTb[:, 0, :], po[0][:, :]), reads=[po[0]], writes=[oTb])
        P.op("act", lambda e: e.copy(oTb[:, 1, :], po[1][:, :]), reads=[po[1]], writes=[oTb])
        P.op("dve", lambda e: e.reciprocal(lacc[:, hh * 4:(hh + 1) * 4], plpa[:, 0:4]), reads=[plpa], writes=[lacc])

        def ao(e):
            for h4 in range(4):
                h = hh * 4 + h4
                for ch in range(2):
                    ins = e.matmul(plpa[:, 256 + h4 * 64:256 + (h4 + 1) * 64], oTb[:, ch, h4 * 128:(h4 + 1) * 128], wuv[:, ch, h, :], start=(ch == 0), stop=(ch == 1))
            return ins
        P.op("pe", ao, reads=[oTb, wuv], writes=[plpa])
        for h4 in range(4):
            h = hh * 4 + h4
            P.op("dve", lambda e, h=h, h4=h4: e.tensor_scalar(attn[:, h * 64:(h + 1) * 64], plpa[:, 256 + h4 * 64:256 + (h4 + 1) * 64], lacc[:, h:h + 1], None, op0=ALU.mult),
                 reads=[plpa, lacc], writes=[attn])

    def attend_qk(L, hh, kt):
        pst = PB[2]
        pTb = pT[kt % 2]
        rb = 32 * (kt % 3)
        rc = (kt // 3) * 128

        def qk(e):
            e.matmul(pst[:, :], cT[:, 0, kt * 128:(kt + 1) * 128], qabsT[:, 0, hh * 512:(hh + 1) * 512], start=True, stop=False)
            e.matmul(pst[:, :], cT[:, 1, kt * 128:(kt + 1) * 128], qabsT[:, 1, hh * 512:(hh + 1) * 512], start=False, stop=False)
            return e.matmul(pst[:, :], cTr[rb:rb + 32, rc:rc + 128], qpeT[rb:rb + 32, hh * 512:(hh + 1) * 512], start=False, stop=True)
        P.op("pe", qk, reads=[cT, cTr, qabsT, qpeT], writes=[pst])
        P.op("act", lambda e: e.activation(pTb[:, :], pst[:, :], AF.Exp, scale=SM_SCALE), reads=[pst], writes=[pTb])
        if kt == L:
            P.op("dve", lambda e: e.tensor_tensor(pTb.t[:, :].rearrange("p (h j) -> p h j", j=128), pTb.t[:, :].rearrange("p (h j) -> p h j", j=128),
                                                  trilb[:, :, :], ALU.mult), reads=[pTb, trilb], writes=[pTb])
        if kt == 0:
            P.op("dve", lambda e: e.tensor_scalar(pTb[:, :], pTb[:, :], m0[:, 0:1], None, op0=ALU.mult), reads=[pTb, m0], writes=[pTb])

    def attend_pv(L, hh, kt, po, plpa):
        pTb = pT[kt % 2]

        def pv(e):
            e.matmul(po[0][:, :], cnat[:, kt, 0:128], pTb[:, :], start=False, stop=(kt == L), skip_group_check=True)
            e.matmul(po[1][:, :], cnat[:, kt, 128:256], pTb[:, :], start=False, stop=(kt == L), skip_group_check=True)
            for h4 in range(4):
                ins = e.matmul(plpa[:, h4:h4 + 1], pTb[:, h4 * 128:(h4 + 1) * 128], onesb[:, 0:1], start=False, stop=(kt == L), skip_group_check=True)
            return ins
        P.op("pe", pv, reads=[cnat, pTb, onesb], writes=[po[0], po[1], plpa])

    def yT_view(cc, is_sample):
        if is_sample:
            return yTs[cc].t[:, :].rearrange("p (b t) -> p b t", t=8)
        return yTs[cc][:, :]

    def conv_post(xslot, g_in_ap_fn, gsrc, conv_is_sample):
        for cc in range(4):
            P.op("dve", lambda e, cc=cc: e.tensor_scalar(yT_view(cc, conv_is_sample), g_in_ap_fn(cc, 0), wdw[:, cc, 0:1], gc["b_dw"][:, cc:cc + 1], op0=ALU.mult, op1=ALU.add),
                 reads=[gsrc, wdw, gc["b_dw"]], writes=[yTs[cc]])
        for k in range(1, CW):
            for cc in range(4):
                P.op("dve", lambda e, cc=cc, k=k: e.scalar_tensor_tensor(yT_view(cc, conv_is_sample), g_in_ap_fn(cc, k), wdw[:, cc, k:k + 1], yT_view(cc, conv_is_sample), ALU.mult, ALU.add),
                     reads=[gsrc, wdw, yTs[cc]], writes=[yTs[cc]])
        pc = PB[0]

        def tr(e):
            for cc in range(4):
                ins = e.transpose(pc[:, cc * 128:(cc + 1) * 128], yTs[cc][:, :], ident[:])
            return ins
        P.op("pe", tr, reads=yTs + [ident], writes=[pc])
        P.op("act", lambda e: e.copy(ydw[:, :], pc[:, :]), reads=[pc], writes=[ydw])
        P.op("dve", lambda e: e.reduce_sum(stat[:, 0:1], ydw[:, :], AX.X), reads=[ydw], writes=[stat])
        P.op("dve", lambda e: e.tensor_scalar(stat[:, 1:2], stat[:, 0:1], -1.0 / CD, None, op0=ALU.mult), reads=[stat], writes=[stat])
        P.op("dve", lambda e: e.tensor_scalar(ydw[:, :], ydw[:, :], stat[:, 1:2], None, op0=ALU.add), reads=[ydw, stat], writes=[ydw])
        ssb = next_ss()
        rstd_of(ydw[:, :], CD, [ydw], ssb)
        P.op("dve", lambda e: e.scalar_tensor_tensor(ydw[:, :], ydw[:, :], ssb[:, 3:4], g_cn_b[:, :], ALU.mult, ALU.mult), reads=[ydw, ssb, g_cn_b], writes=[ydw])
        P.op("dve", lambda e: e.tensor_add(ydw[:, :], ydw[:, :], b_cn_b[:, :]), reads=[ydw, b_cn_b], writes=[ydw])
        P.op("act", lambda e: e.activation(cvo[:, :], ydw[:, :], AF.Silu), reads=[ydw], writes=[cvo])
        ssa = next_ss()
        rstd_of(attn[:, :], CD, [attn], ssa)
        P.op("dve", lambda e: e.tensor_scalar(mix[:, 0:512], attn[:, :], ssa[:, 3:4], None, op0=ALU.mult), reads=[attn, ssa], writes=[mix])
        ssc = next_ss()
        rstd_of(cvo[:, :], CD, [cvo], ssc)
        P.op("dve", lambda e: e.tensor_scalar(mix[:, 512:1024], cvo[:, :], ssc[:, 3:4], None, op0=ALU.mult), reads=[cvo, ssc], writes=[mix])

        def trm(e):
            for c in range(8):
                ins = e.transpose(PT0[:, c * 128:(c + 1) * 128], mix[:, c * 128:(c + 1) * 128], identb[:])
            return ins
        P.op("pe", trm, reads=[mix, identb], writes=[PT0])
        P.op("act", lambda e: e.copy(mixT[:, :, :], PT0.t[:].rearrange("p (c j) -> p c j", j=128)), reads=[PT0], writes=[mixT])
        pd0, pd1 = PB[2], PB[3]
        for k in range(8):
            wk_ = piece("w_out", k * 128)

            def mmo(e, k=k, wk_=wk_):
                e.matmul(pd0[:, :], mixT[:, k, :], wk_[:, 0:512], start=(k == 0), stop=(k == 7))
                return e.matmul(pd1[:, :], mixT[:, k, :], wk_[:, 512:1024], start=(k == 0), stop=(k == 7))
            P.op("pe", mmo, reads=[mixT, wk_], writes=[pd0, pd1])
        for dh, pd in enumerate((pd0, pd1)):
            P.op("dve", lambda e, dh=dh, pd=pd: e.tensor_add(X[xslot][:, dh * 512:(dh + 1) * 512], X[xslot][:, dh * 512:(dh + 1) * 512], pd[:, :]),
                 reads=[pd, X[xslot]], writes=[X[xslot]])

    def final_out(xslot, dst_ap):
        ssb = next_ss()
        rstd_of(X[xslot][:, :], D, [X[xslot]], ssb)
        P.op("dve", lambda e: e.scalar_tensor_tensor(ybuf[:, :], X[xslot][:, :], ssb[:, 3:4], g_fin_b[:, :], ALU.mult, ALU.mult),
             reads=[X[xslot], ssb, g_fin_b], writes=[ybuf])
        out_dma(dst_ap, ybuf, ybuf[:, :])

    import os
    if os.environ.get("KSTOP", "0") == "1":
        P.finish()
        with nc.Block() as block:
            block.tensor(lambda e: P.replay("pe", e)); block.scalar(lambda e: P.replay("act", e)); block.vector(lambda e: P.replay("dve", e))
            block.gpsimd(lambda e: P.replay("pool", e)); block.sync(lambda e: P.replay("sp", e))
        es.close()
        return nc
    dma_in(X[XS], X[XS][:, :], xs.ap())
    ffn([XS], "w1")
    norm_T(X[XS], 0)
    dcp = Buf(None)
    P.op("pool", lambda e: e.dma_start(out=conv_s.ap()[:, 0:HALO - 8, :], in_=state.ap()[:, 8:HALO, :]), dma=dcp, is_out=True)
    in_proj([XS], [XS], 0, cs_s, True)
    pg_ = PB[2]

    def trg(e):
        for cc in range(4):
            ins = e.transpose(pg_[:, cc * 128:(cc + 1) * 128], gtok[:, cc * 128:(cc + 1) * 128], ident[:])
        return ins
    P.op("pe", trg, reads=[gtok, ident], writes=[pg_])
    P.op("act", lambda e: e.copy(gtk2[:, :], pg_[:, :]), reads=[pg_], writes=[gtk2])
    for b in range(16):
        out_dma(conv_s.ap()[b, HALO - 8:HALO, :], gtk2, gtk2[b * 8:(b + 1) * 8, :])
    q_path(0, cs_sT, 0, qabsT_s, qpeT_s)

    pmod = sb("pmod", [128, 1], I32)
    dma_in(pmod, pmod[:], pmod_d.ap())
    ptx = sb("ptx", [128, 16, NG8], I32)
    for i8 in range(8):
        src = bass.AP(tensor=ptab, offset=i8, ap=[[0, 16], [NPG, 16], [8, NG8]])
        P.op("sp", lambda e, i8=i8, src=src: e.dma_start(out=ptx[16 * i8:16 * i8 + 16, :, :], in_=src, allow_slow_non_contiguous=True), writes=[ptx], dma=ptx)
    gidx = sb("gidx", [128, 16, NG8], I32)
    P.op("pool", lambda e: e.tensor_scalar(gidx[:], ptx[:], 16, pmod[:, 0:1], op0=ALU.mult, op1=ALU.add), reads=[ptx, pmod], writes=[gidx])

    units = []
    for b in range(16):
        units.append((b, -1))
        for g in range(NG8):
            units.append((b, g))
    sstate = {"i": 0, "gi": 0, "prev": None, "done": False}

    def qviews(b):
        qa = [qabsT_s.t[:, ch, :].rearrange("p (h q) -> p h q", q=128)[:, :, b * 8:(b + 1) * 8] for ch in range(2)]
        qpv = qpeT_s.t[:, :].rearrange("p (h q) -> p h q", q=128)[:, :, b * 8:(b + 1) * 8]
        return qa, qpv

    def s_gather(b, g):
        cgb, rgb = cg[sstate["gi"] % NCG], rg[sstate["gi"] % NCG]
        sstate["gi"] += 1
        P.op("pool", lambda e: e.indirect_dma_start(
            out=cgb[:, :, :].rearrange("p t c -> p (t c)"), out_offset=None, in_=ckv.ap(),
            in_offset=bass.IndirectOffsetOnAxis(ap=gidx[:, b, g:g + 1], axis=0)), reads=[gidx], writes=[cgb], dma=cgb)
        P.op("pool", lambda e: e.indirect_dma_start(
            out=rgb[:, :, :].rearrange("p t c -> p (t c)"), out_offset=None, in_=ckr.ap(),
            in_offset=bass.IndirectOffsetOnAxis(ap=gidx[:, b, g:g + 1], axis=0)), reads=[gidx], writes=[rgb], dma=rgb)
        return cgb, rgb

    def s_T(cgb, rgb, pr):
        cTb, rTb = cTs[pr % 2], rTs[pr % 2]

        def tr(e):
            for t2 in range(2):
                t = pr * 2 + t2
                e.transpose(SPT[:, t2 * 256:t2 * 256 + 128], cgb[:, t, 0:128], identb[:])
                e.transpose(SPT[:, t2 * 256 + 128:t2 * 256 + 256], cgb[:, t, 128:256], identb[:])
            return e.transpose(SPT[0:64, 512:640], rgb[:, pr * 2:pr * 2 + 2, :].rearrange("p t c -> p (t c)"), identb[:])
        P.op("pe", tr, reads=[cgb, rgb, identb], writes=[SPT])
        if pr % 2 == 0:
            P.op("dve", lambda e: e.tensor_copy(cTb[:, :, :].rearrange("p t c -> p (t c)"), SPT[:, 0:512]), reads=[SPT], writes=[cTb])
            P.op("dve", lambda e: e.tensor_copy(rTb[0:64, :], SPT[0:64, 512:640]), reads=[SPT], writes=[rTb])
        else:
            P.op("act", lambda e: e.copy(cTb[:, :, :].rearrange("p t c -> p (t c)"), SPT[:, 0:512]), reads=[SPT], writes=[cTb])
            P.op("act", lambda e: e.copy(rTb[0:64, :], SPT[0:64, 512:640]), reads=[SPT], writes=[rTb])

    def s_QK(b, pr):
        cTb, rTb = cTs[pr % 2], rTs[pr % 2]
        qa, qpv = qviews(b)

        def qk(e):
            for t2 in range(2):
                t = pr * 2 + t2
                e.matmul(SPST[:, t * 64:(t + 1) * 64], cTb[:, t2, 0:128], qa[0], start=True, stop=False)
                e.matmul(SPST[:, t * 64:(t + 1) * 64], cTb[:, t2, 128:256], qa[1], start=False, stop=False)
                ins = e.matmul(SPST[:, t * 64:(t + 1) * 64], rTb[32 * t2:32 * t2 + 32, :], qpv[32 * t2:32 * t2 + 32], start=False, stop=True)
            return ins
        P.op("pe", qk, reads=[cTb, rTb, qabsT_s, qpeT_s], writes=[SPST])

    def s_PV(pv_info, lo, hi, final):
        b, cgb, pTb, new = pv_info

        def pv(e):
            for t in range(lo, hi):
                v0 = cnat_s[:, 0:128] if new else cgb[:, t, 0:128]
                v1 = cnat_s[:, 128:256] if new else cgb[:, t, 128:256]
                e.matmul(SACC[:, 0:64], v0, pTb[:, t, :], start=False, stop=False, skip_group_check=True)
                e.matmul(SACC[:, 64:128], v1, pTb[:, t, :], start=False, stop=False, skip_group_check=True)
                ins = e.matmul(SACC[:, 128:192], onesb[:, :], pTb[:, t, :], start=False, stop=final, skip_group_check=True)
            return ins
        P.op("pe", pv, reads=[pTb, onesb] + ([cnat_s] if new else [cgb]), writes=[SACC])

    def s_finalize(b):
        P.op("act", lambda e: e.copy(osb[:, :], SACC[:, 0:192]), reads=[SACC], writes=[osb])
        P.op("dve", lambda e: e.reciprocal(osb[:, 128:192], osb[:, 128:192]), reads=[osb], writes=[osb])
        for ch in range(2):
            P.op("dve", lambda e, ch=ch: e.tensor_tensor(
                oTs.t[:, ch, :].rearrange("p (h q) -> p h q", q=128)[:, :, b * 8:(b + 1) * 8],
                osb.t[:, ch * 64:(ch + 1) * 64].rearrange("p (h t) -> p h t", t=8),
                osb.t[:, 128:192].rearrange("p (h t) -> p h t", t=8), ALU.mult), reads=[osb], writes=[oTs])

    def s_zero():
        P.op("pe", lambda e: e.matmul(SACC[:, 0:192], zerob[:, 0:128], zerob[:, 0:192], start=True, stop=True), reads=[zerob], writes=[SACC])

    def s_prev_pv(lo, hi):
        pi = sstate["prev"]
        if pi is None:
            return
        b, cgb, pTb, new, nsub, is_last_of_batch = pi
        lo2, hi2 = min(lo, nsub), min(hi, nsub)
        if hi2 > lo2:
            s_PV((b, cgb, pTb, new), lo2, hi2, final=(is_last_of_batch and hi2 == nsub))
        if hi >= 8:
            if is_last_of_batch:
                s_finalize(b)
                s_zero()
            sstate["prev"] = None

    gath = {}
    gq = {"next": 0}
    gunits = [i for i, (b_, g_) in enumerate(units) if g_ >= 0]

    def ensure_gathers(upto_unit):
        while gq["next"] < len(gunits) and gunits[gq["next"]] <= upto_unit:
            ui = gunits[gq["next"]]
            gq["next"] += 1
            gath[ui] = s_gather(units[ui][0], units[ui][1])

    def stream():
        s_zero()
        ensure_gathers(2)
        for i, (b, g) in enumerate(units):
            sstate["i"] = i + 1
            is_last = (i + 1 == len(units)) or units[i + 1][0] != b
            pTb = pTs[i % 2]
            if g < 0:
                s_prev_pv(0, 8)
                qa, qpv = qviews(b)

                def qk(e, qa=qa, qpv=qpv):
                    e.matmul(SPST[:, 0:64], cT_s[:, 0, :], qa[0], start=True, stop=False)
                    e.matmul(SPST[:, 0:64], cT_s[:, 1, :], qa[1], start=False, stop=False)
                    return e.matmul(SPST[:, 0:64], cTr_s[0:32, :], qpv[0:32], start=False, stop=True)
                P.op("pe", qk, reads=[cT_s, cTr_s, qabsT_s, qpeT_s], writes=[SPST])
                P.op("act", lambda e, pTb=pTb: e.activation(pTb[:, 0, :], SPST[:, 0:64], AF.Exp, scale=SM_SCALE), reads=[SPST], writes=[pTb])
                P.op("dve", lambda e, pTb=pTb, b=b: e.tensor_tensor(pTb[:, 0, :], pTb[:, 0, :], smask[:, b * 64:(b + 1) * 64], ALU.mult), reads=[pTb, smask], writes=[pTb])
                sstate["prev"] = (b, None, pTb, True, 1, is_last)
                yield
                continue
            cgb, rgb = gath.pop(i)
            s_T(cgb, rgb, 0)
            s_prev_pv(0, 4)
            yield
            s_T(cgb, rgb, 1)
            s_QK(b, 0)
            s_prev_pv(4, 8)
            ensure_gathers(i + 3)
            yield
            s_T(cgb, rgb, 2)
            s_QK(b, 1)
            yield
            s_T(cgb, rgb, 3)
            s_QK(b, 2)
            yield
            s_QK(b, 3)
            P.op("act", lambda e, pTb=pTb: e.activation(pTb[:, :, :].rearrange("p t q -> p (t q)"), SPST[:, :], AF.Exp, scale=SM_SCALE), reads=[SPST], writes=[pTb])
            sstate["prev"] = (b, cgb, pTb, False, 8, is_last)
            yield
        s_prev_pv(0, 8)
        sstate["done"] = True

    sgen = stream()

    def pump():
        if sstate["done"]:
            return
        try:
            next(sgen)
        except StopIteration:
            sstate["done"] = True

    def sample_tail():
        while not sstate["done"]:
            pump()
        pa_ = PB[2]

        def aos(e):
            for h in range(NH):
                for ch in range(2):
                    ins = e.matmul(pa_[:, h * 64:(h + 1) * 64], oTs[:, ch, h * 128:(h + 1) * 128], wuv[:, ch, h, :], start=(ch == 0), stop=(ch == 1))
            return ins
        P.op("pe", aos, reads=[oTs, wuv], writes=[pa_])
        P.op("act", lambda e: e.copy(attn[:, :], pa_[:, :]), reads=[pa_], writes=[attn])

        def gin_sample(cc, k):
            return gT_s.t[:, cc, :].rearrange("p (b s) -> p b s", s=38)[:, :, k:k + 8]
        conv_post(XS, gin_sample, gT_s, True)
        ffn([XS], "w2")
        final_out(XS, y_s.ap())

    TAIL_AFTER = max(NGRP - 2, 0)
    import os
    STOP = int(os.environ.get("KSTOP", "0"))
    SEQ = os.environ.get("KSEQ", "0") == "1"
    if STOP == 3 or SEQ:
        pump_ = None
    else:
        pump_ = pump
    if SEQ:
        sample_tail()
    for grp in range(NGRP if STOP not in (1, 2) else 0):
        for i in range(4):
            lt = grp * 4 + i
            dma_in(X[i], X[i][:, :], xp.ap()[lt * 128:(lt + 1) * 128, :])
        ffn([0, 1, 2, 3], "w1", pump_)
        for i in range(4):
            norm_T(X[i], i)
        if grp == 0:
            P.op("pool", lambda e: e.memset(gT[:, :, 0:HALO], 0.0), writes=[gT])
        else:
            P.op("pool", lambda e: e.tensor_copy(gT[:, :, 0:HALO], gT[:, :, 512:512 + HALO]), reads=[gT], writes=[gT])
        in_proj([0, 1, 2, 3], [1, 3], grp * 4, cs_p, False)
        for i in (1, 3):
            L = grp * 4 + i
            own_idx = L // 2
            q_path(i, cs_pT, own_idx, qabsT, qpeT)
            attend_prompt(L, None)

            def gin_p(cc, k, i=i):
                return gT[:, cc, i * 128 + k:i * 128 + k + 128]
            conv_post(i, gin_p, gT, False)
            if L == NT - 1:
                def trp2(e, i=i):
                    for cc in range(4):
                        ins = e.transpose(PT0[0:HALO, cc * 128:(cc + 1) * 128], gT[:, cc, HALO + i * 128 + 128 - HALO:HALO + i * 128 + 128], identb[:])
                    return ins
                P.op("pe", trp2, reads=[gT, identb], writes=[PT0])
                P.op("act", lambda e: e.copy(gtk2[0:HALO, :], PT0[0:HALO, 0:512]), reads=[PT0], writes=[gtk2])
                out_dma(conv_p.ap(), gtk2, gtk2[0:HALO, :])
        ffn([1, 3], "w2", pump_)
        for i in (1, 3):
            own_idx = (grp * 4 + i) // 2
            final_out(i, y_p.ap()[own_idx * 128:(own_idx + 1) * 128, :])
        if grp == TAIL_AFTER and STOP not in (3, 4) and not SEQ:
            sample_tail()

    P.finish()
    with nc.Block() as block:
        @block.tensor
        def _(e):
            P.replay("pe", e)

        @block.scalar
        def _(e):
            P.replay("act", e)

        @block.vector
        def _(e):
            P.replay("dve", e)

        @block.gpsimd
        def _(e):
            P.replay("pool", e)

        @block.sync
        def _(e):
            P.replay("sp", e)
    print('SBUF bytes remaining', nc.sbuf_bytes_remaining, 'nsem', P.nsem, {k: len(v) for k, v in P.lists.items()}, 'units pumped', sstate["i"], len(units))
    es.close()
    return nc


def rope_tab(pos):
    inv = (10000.0 ** (-np.arange(0, 32, 2, dtype=np.float32) / np.float32(32))).astype(np.float32)
    ang = pos.astype(np.float32)[:, None] * inv[None, :]
    c, s = np.cos(ang).astype(np.float32), np.sin(ang).astype(np.float32)
    return np.concatenate([c, c, s, s], axis=1)


_CACHE = {}
DBG = False
DBG_OUT = {}


def kernel(x_prompt, x_sample, cache_kv_latent, cache_k_rope, state_conv, page_table,
           g_ffn1, w1_gate, w1_up, w1_down, g_mix, w_in, g_q, w_q_b, g_kv, w_kv_b,
           w_dw, b_dw, g_cn, b_cn, g_out_attn, g_out_conv, w_out,
           g_ffn2, w2_gate, w2_up, w2_down, g_final):
    f = lambda a: np.ascontiguousarray(np.asarray(a))
    x_prompt = f(x_prompt); x_sample = f(x_sample)
    B, S, _ = x_prompt.shape
    DB, T, _ = x_sample.shape
    NPHYS = cache_kv_latent.shape[1]
    NPG = page_table.shape[1]
    past = NPG * 128
    NTILE = S // 128
    NT = NTILE
    assert DB == 128 and T == 8 and B == 4
    key = (NT, NPG, NPHYS)
    if key not in _CACHE:
        _CACHE[key] = build(NT, NPG, NPHYS, DBG)
    nc = _CACHE[key]
    ckv = f(cache_kv_latent).reshape(NPHYS * 16, 8 * KVR)
    ckr = f(cache_k_rope).reshape(NPHYS * 16, 8 * RD)
    page_table = f(page_table).astype(np.int32)
    common = dict(
        ckv=ckv, ckr=ckr,
        w1g=f(w1_gate)[0], w1u=f(w1_up)[0], w1d=f(w1_down)[0], w2g=f(w2_gate)[0], w2u=f(w2_up)[0], w2d=f(w2_down)[0],
        w_in=f(w_in)[0], w_qb=f(w_q_b)[0].reshape(QR, NH * 96), w_kvb=f(w_kv_b)[0].reshape(KVR, NH * 128),
        w_dw=f(w_dw)[0], w_out=f(w_out)[0],
        g_ffn1=f(g_ffn1), g_mix=f(g_mix), g_ffn2=f(g_ffn2), g_final=f(g_final).reshape(1, D), g_q=f(g_q), g_kv=f(g_kv),
        b_dw=f(b_dw), g_cn=f(g_cn), b_cn=f(b_cn), g_oa=f(g_out_attn), g_oc=f(g_out_conv),
        ident=np.eye(128, dtype=np.float32),
        tril=np.triu(np.ones((128, 128), np.float32)),
        pmod=(np.arange(128, dtype=np.int32) % 16).reshape(128, 1),
    )
    k = np.arange(128)
    sm = np.zeros((128, 16, NH, 8), np.float32)
    for b in range(16):
        for t in range(8):
            sm[(k // 8 == b) & (k % 8 <= t), b, :, t] = 1.0
    common["smask"] = sm.reshape(128, 16 * 64)
    cs_s = rope_tab(past + np.arange(8))
    cs_s128 = np.tile(cs_s, (16, 1))
    common["cs_s"] = cs_s128
    common["cs_sT"] = np.ascontiguousarray(np.concatenate([cs_s128[:, 0:32].T, cs_s128[:, 32:64].T], axis=0))
    sel = np.zeros((128, 64), np.float32)
    for r_ in range(2):
        sel[64 + np.arange(32), 32 * r_ + np.arange(32)] = 1.0
        sel[96 + np.arange(32), 32 * r_ + np.arange(32)] = 1.0
    common["selsum"] = sel
    in_maps = []
    for c in range(8):
        b, par = c // 2, c % 2
        xb = x_prompt[b].reshape(NTILE, 128, D)
        if par == 0:
            gt = np.arange(-1, NTILE - 1)
        else:
            gt = np.arange(0, NTILE)
        xl = np.zeros((NT, 128, D), np.float32)
        for lt in range(NT):
            if gt[lt] >= 0:
                xl[lt] = xb[gt[lt]]
        pos = (gt[:, None] * 128 + np.arange(128)[None, :]).reshape(-1)
        pos = np.maximum(pos, 0)
        cs = rope_tab(pos)
        own = cs.reshape(NT, 128, 64)[1::2].reshape(-1, 64)
        m = dict(common)
        m["xp"] = xl.reshape(NT * 128, D)
        m["xs"] = x_sample[16 * c:16 * c + 16].reshape(128, D)
        m["ptab"] = page_table[16 * c:16 * c + 16]
        m["state"] = f(state_conv)[0, 16 * c:16 * c + 16]
        m["cs_p"] = cs
        m["cs_pT"] = np.ascontiguousarray(np.concatenate([own[:, 0:32].T, own[:, 32:64].T], axis=0))
        m["m0"] = np.full((128, 1), float(par), np.float32)
        in_maps.append(m)
    res = run_bass_kernel_spmd(nc, in_maps, core_ids=list(range(8)))
    R = res.results
    y_prompt = np.zeros((B, S, D), np.float32)
    new_kv_p = np.zeros((1, B, NTILE, 128, KVR), np.float32)
    new_kr_p = np.zeros((1, B, NTILE, 128, RD), np.float32)
    new_cv_p = np.zeros((1, B, HALO, CD), np.float32)
    y_sample = np.zeros((DB, T, D), np.float32)
    new_kv_s = np.zeros((1, DB, T, KVR), np.float32)
    new_kr_s = np.zeros((1, DB, T, RD), np.float32)
    new_cv_s = np.zeros((1, DB, HALO, CD), np.float32)
    for c in range(8):
        b, par = c // 2, c % 2
        r = R[c]
        gown = np.arange(par, NTILE, 2)
        yp = np.asarray(r["y_p"]).reshape(NT // 2, 128, D)
        kvp = np.asarray(r["kv_p"]).reshape(NT // 2, 128, KVR)
        krp = np.asarray(r["kr_p"]).reshape(NT // 2, 128, RD)
        for j, g in enumerate(gown):
            y_prompt[b, g * 128:(g + 1) * 128] = yp[j]
            new_kv_p[0, b, g] = kvp[j]
            new_kr_p[0, b, g] = krp[j]
        if par == 1:
            new_cv_p[0, b] = np.asarray(r["conv_p"])
        y_sample[16 * c:16 * c + 16] = np.asarray(r["y_s"]).reshape(16, 8, D)
        new_kv_s[0, 16 * c:16 * c + 16] = np.asarray(r["kv_s"]).reshape(16, 8, KVR)
        new_kr_s[0, 16 * c:16 * c + 16] = np.asarray(r["kr_s"]).reshape(16, 8, RD)
        new_cv_s[0, 16 * c:16 * c + 16] = np.asarray(r["conv_s"])
    return (y_prompt, y_sample, new_kv_p, new_kr_p, new_cv_p, new_kv_s, new_kr_s, new_cv_s)
```
